# Optimizing a Trainium2 kernel written in Bass

```python
import jax, jax.numpy as jnp
from jax import lax
import numpy as np

D_MODEL = 4096
BATCH = 4
SEQ = 2048
DEPTH = 1

D_MIX = D_MODEL
HEAD_DIM = 128
D_A = D_MIX // 2
D_B = D_MIX - D_A
N_HEADS_A = D_A // HEAD_DIM
N_BLOCKS_B = D_B // HEAD_DIM
CHUNK = 128
CONV_WIDTH = 4
LRU_C = 8.0
D_FF = ((8 * D_MODEL + 3 * 256 - 1) // (3 * 256)) * 256
D_IN = 2 * D_A + 2 * D_B
EPS = 1e-6

kernel_name = "hybrid_gmlp_rglru_sandwich_block"


def rmsnorm(x, g):
    x32 = x.astype(jnp.float32)
    y = x32 * lax.rsqrt(jnp.mean(x32 * x32, axis=-1, keepdims=True) + EPS)
    return (y * g.astype(jnp.float32)).astype(x.dtype)


def gmlp_mixer(u, v, v_norm_g, w_spatial, b_spatial):
    B, S, _ = v.shape
    u = jax.nn.gelu(u)
    v = jax.nn.gelu(v).reshape(B, S, N_HEADS_A, HEAD_DIM)
    v = rmsnorm(v, v_norm_g.reshape(N_HEADS_A, HEAD_DIM))
    vc = v.reshape(B, S // CHUNK, CHUNK, N_HEADS_A, HEAD_DIM)
    causal = jnp.tril(jnp.ones((CHUNK, CHUNK), dtype=bool))
    ws = jnp.where(causal[None], w_spatial, jnp.zeros_like(w_spatial))
    mixed = jnp.einsum('hts,bcshd->bcthd', ws, vc) + b_spatial.T[None, None, :, :, None]
    return u * mixed.reshape(B, S, D_A)


def causal_depthwise_conv(x, w_conv, b_conv):
    S = x.shape[1]
    xpad = jnp.pad(x, ((0, 0), (CONV_WIDTH - 1, 0), (0, 0)))
    out = b_conv
    for k in range(CONV_WIDTH):
        out = out + xpad[:, k:k + S, :] * w_conv[k]
    return out


def block_diag_linear(x, w, b):
    B, S, _ = x.shape
    xb = x.reshape(B, S, N_BLOCKS_B, HEAD_DIM)
    return jnp.einsum('bsnd,nde->bsne', xb, w).reshape(B, S, D_B) + b


def rglru_mixer(gate, xr, w_conv, b_conv, w_r, b_r, w_i, b_i, lru_lambda):
    xc = causal_depthwise_conv(xr, w_conv, b_conv)
    r = jax.nn.sigmoid(block_diag_linear(xc, w_r, b_r)).astype(jnp.float32)
    i = jax.nn.sigmoid(block_diag_linear(xc, w_i, b_i)).astype(jnp.float32)
    log_a = -LRU_C * r * jax.nn.softplus(-lru_lambda.astype(jnp.float32))
    a = jnp.exp(log_a)
    mult = jnp.sqrt(jnp.maximum(1.0 - jnp.exp(2.0 * log_a), 1e-12))
    bterm = mult * (i * xc.astype(jnp.float32))

    def combine(left, right):
        a_l, b_l = left
        a_r, b_r_ = right
        return a_l * a_r, a_r * b_l + b_r_

    _, h = lax.associative_scan(combine, (a, bterm), axis=1)
    return h.astype(xr.dtype) * jax.nn.gelu(gate)


def setup_inputs(seed: int = 0) -> dict:
    key = jax.random.key(seed)
    ks = jax.random.split(key, 24)
    f32 = jnp.float32

    def nrm(k, shape, scale):
        return jax.random.normal(k, shape, f32) * scale

    def gain(k, shape):
        return 1.0 + 0.02 * jax.random.normal(k, shape, f32)

    L = DEPTH
    x = jax.random.normal(ks[0], (BATCH, SEQ, D_MODEL), f32)
    pre_mix_g = gain(ks[1], (L, D_MODEL))
    w_in = nrm(ks[2], (L, D_MODEL, D_IN), D_MODEL ** -0.5)
    gmlp_v_norm_g = gain(ks[3], (L, D_A))
    w_spatial = nrm(ks[4], (L, N_HEADS_A, CHUNK, CHUNK), CHUNK ** -0.5)
    b_spatial = 1.0 + 0.01 * jax.random.normal(ks[5], (L, N_HEADS_A, CHUNK), f32)
    w_conv = nrm(ks[6], (L, CONV_WIDTH, D_B), CONV_WIDTH ** -0.5)
    b_conv = nrm(ks[7], (L, D_B), 0.01)
    w_r = nrm(ks[8], (L, N_BLOCKS_B, HEAD_DIM, HEAD_DIM), HEAD_DIM ** -0.5)
    b_r = nrm(ks[9], (L, D_B), 0.01)
    w_i = nrm(ks[10], (L, N_BLOCKS_B, HEAD_DIM, HEAD_DIM), HEAD_DIM ** -0.5)
    b_i = nrm(ks[11], (L, D_B), 0.01)
    a_c = jax.random.uniform(ks[12], (L, D_B), f32, 0.9, 0.999)
    a0 = a_c ** (1.0 / LRU_C)
    lru_lambda = jnp.log(a0) - jnp.log1p(-a0)
    out_norm_a_g = gain(ks[13], (L, D_A))
    out_norm_b_g = gain(ks[14], (L, D_B))
    w_out = nrm(ks[15], (L, D_MIX, D_MODEL), D_MIX ** -0.5)
    post_mix_g = gain(ks[16], (L, D_MODEL))
    pre_ffn_g = gain(ks[17], (L, D_MODEL))
    w_ffn_in = nrm(ks[18], (L, D_MODEL, 2 * D_FF), D_MODEL ** -0.5)
    w_ffn_out = nrm(ks[19], (L, D_FF, D_MODEL), D_FF ** -0.5)
    post_ffn_g = gain(ks[20], (L, D_MODEL))
    return {"x": x, "pre_mix_g": pre_mix_g, "w_in": w_in, "gmlp_v_norm_g": gmlp_v_norm_g,
            "w_spatial": w_spatial, "b_spatial": b_spatial, "w_conv": w_conv, "b_conv": b_conv,
            "w_r": w_r, "b_r": b_r, "w_i": w_i, "b_i": b_i, "lru_lambda": lru_lambda,
            "out_norm_a_g": out_norm_a_g, "out_norm_b_g": out_norm_b_g, "w_out": w_out,
            "post_mix_g": post_mix_g, "pre_ffn_g": pre_ffn_g, "w_ffn_in": w_ffn_in,
            "w_ffn_out": w_ffn_out, "post_ffn_g": post_ffn_g}


def reference(x, pre_mix_g, w_in, gmlp_v_norm_g, w_spatial, b_spatial, w_conv, b_conv,
              w_r, b_r, w_i, b_i, lru_lambda, out_norm_a_g, out_norm_b_g, w_out,
              post_mix_g, pre_ffn_g, w_ffn_in, w_ffn_out, post_ffn_g):
    for l in range(DEPTH):
        h = rmsnorm(x, pre_mix_g[l])
        proj = jnp.einsum('bsd,de->bse', h, w_in[l])
        u, v, gate, xr = jnp.split(proj, [D_A, 2 * D_A, 2 * D_A + D_B], axis=-1)
        y_a = gmlp_mixer(u, v, gmlp_v_norm_g[l], w_spatial[l], b_spatial[l])
        y_b = rglru_mixer(gate, xr, w_conv[l], b_conv[l], w_r[l], b_r[l],
                          w_i[l], b_i[l], lru_lambda[l])
        y = jnp.concatenate([rmsnorm(y_a, out_norm_a_g[l]),
                             rmsnorm(y_b, out_norm_b_g[l])], axis=-1)
        y = jnp.einsum('bse,ed->bsd', y, w_out[l])
        x = x + rmsnorm(y, post_mix_g[l])
        h = rmsnorm(x, pre_ffn_g[l])
        gu = jnp.einsum('bsd,df->bsf', h, w_ffn_in[l])
        g, up = jnp.split(gu, 2, axis=-1)
        f = jnp.einsum('bsf,fd->bsd', jax.nn.silu(g) * up, w_ffn_out[l])
        x = x + rmsnorm(f, post_ffn_g[l])
    return x
```

```python
import contextlib
import numpy as np
import concourse.bass as bass
import concourse.mybir as mybir
from concourse.bass_utils import run_bass_kernel_spmd

F32 = mybir.dt.float32
BF16 = mybir.dt.bfloat16
AF = mybir.ActivationFunctionType
ALU = mybir.AluOpType
AX = mybir.AxisListType

D = 4096
NT = 1024
DA = 2048
DB = 2048
DIN = 8192
DFF = 11008
NFB = DFF // 128
KC = D // 128
EPS = 1e-6
C1 = 0.7978845608028654
C3 = C1 * 0.044715
SQC3 = C3 ** 0.5
NV = 224
EPOCH = 3000
STOP_AFTER = None


class Op:
    __slots__ = ("eng", "fn", "deps", "dma", "mark", "sem", "val", "gi")

    def __init__(self, eng, fn, dma):
        self.eng = eng
        self.fn = fn
        self.dma = dma
        self.deps = set()
        self.mark = False
        self.sem = None
        self.val = 0


class Sched:
    def __init__(self):
        self.ops = []
        self.last_w = {}
        self.readers = {}
        self.group_keys = set()
        self.ar = []

    def _alias(self, x):
        if isinstance(x, tuple) and x[0] == "ar":
            if x not in self.ar:
                self.ar.append(x)
            return [y for y in self.ar if y[1] < x[2] and x[1] < y[2]]
        return (x,)

    def op(self, eng, fn, r=(), w=(), dma=None):
        o = Op(eng, fn, dma)
        o.gi = len(self.ops)
        deps = o.deps
        for x0 in r:
            for x in self._alias(x0):
                p = self.last_w.get(x)
                if p is not None:
                    deps.add(p)
        for x0 in w:
            for x in self._alias(x0):
                p = self.last_w.get(x)
                if p is not None:
                    deps.add(p)
                for rd in self.readers.get(x, ()):
                    deps.add(rd)
        for x in r:
            self.readers.setdefault(x, []).append(o)
        for x in w:
            self.last_w[x] = o
            self.readers[x] = []
        deps.discard(o)
        self.ops.append(o)
        return o

    def inherit(self, new, olds):
        rs = self.readers.setdefault(new, [])
        for x in olds:
            p = self.last_w.get(x)
            if p is not None:
                rs.append(p)
            rs.extend(self.readers.get(x, ()))

    def emit(self, nc, stack):
        engs = ["pe", "act", "dve", "pool", "sp"]
        per = {e: [o for o in self.ops if o.eng == e] for e in engs}
        for o in self.ops:
            for p in o.deps:
                if p.dma is None:
                    if p.eng == "pe" and o.eng == "pe":
                        continue
                    p.mark = True
        sems = {}

        def getsem(key):
            if key not in sems:
                sems[key] = stack.enter_context(nc.semaphore("s%d" % len(sems)))
            return sems[key]

        for e in engs:
            cnt = 0
            for o in per[e]:
                if o.dma is None and o.mark:
                    ep, v = divmod(cnt, EPOCH)
                    o.sem = getsem(("eng", e, ep))
                    o.val = v + 1
                    cnt += 1
        dcount = {}
        for o in self.ops:
            if o.dma is not None:
                dcount[o.dma] = dcount.get(o.dma, 0) + 1
                o.sem = getsem(("dma", o.dma))
                o.val = 16 * dcount[o.dma]
        for o in self.ops:
            if o.dma is not None and o.dma in self.group_keys:
                o.val = 16 * dcount[o.dma]

        block = stack.enter_context(nc.Block())

        def run(engobj, ops):
            waited = {}
            for o in ops:
                need = {}
                for p in o.deps:
                    if p.dma is None and p.eng == "pe" and o.eng == "pe":
                        continue
                    k = id(p.sem)
                    if k not in need or need[k][1] < p.val:
                        need[k] = (p.sem, p.val)
                for k, (s, v) in need.items():
                    if waited.get(k, 0) < v:
                        engobj.wait_ge(s, v)
                        waited[k] = v
                ins = o.fn(engobj) if o.fn is not None else None
                if o.dma is not None:
                    ins.then_inc(o.sem, 16)
                elif o.mark:
                    ins.then_inc(o.sem, 1)

        @block.tensor
        def _(e):
            run(e, per["pe"])

        @block.scalar
        def _(e):
            run(e, per["act"])

        @block.vector
        def _(e):
            run(e, per["dve"])

        @block.gpsimd
        def _(e):
            run(e, per["pool"])

        @block.sync
        def _(e):
            run(e, per["sp"])


class T:
    def __init__(self, ap, r):
        self.ap = ap
        self.r = r


def build_program():
    nc = bass.Bass("TRN2", target_bir_lowering=False)
    stack = contextlib.ExitStack()
    S = Sched()

    def din(name, shape, dt=F32):
        return nc.dram_tensor(name, list(shape), dt, kind="ExternalInput").ap()

    x_own = din("x_own", [NT, D])
    x_prev = din("x_prev", [NT, D])
    w_in = din("w_in", [D, DIN])
    w_out = din("w_out", [D, D])
    w_fi = din("w_ffn_in", [D, 2 * DFF])
    w_fo = din("w_ffn_out", [DFF, D])
    pv_d = din("pv", [128, NV])
    flag_d = din("flag", [128, 1])
    ident_d = din("ident", [128, 128])
    tri_d = din("tri", [128, 128])
    wsT_d = din("wsT", [128, 16, 128])
    wr_d = din("wr", [128, 16, 128])
    wi_d = din("wi", [128, 16, 128])
    bsp_d = din("bsp", [1, 2048])
    gv_d = din("gvb", [128, 2048])
    gpm_d = din("gpm", [128, D])
    gpf_d = din("gpf", [128, D])
    out_d = nc.dram_tensor("out", [NT, D], F32, kind="ExternalOutput").ap()
    oscr = nc.dram_tensor("oscr", [NT, D], F32, kind="ExternalOutput").ap()
    fscr = nc.dram_tensor("fscr", [NT, D], F32, kind="ExternalOutput").ap()
    yscr = nc.dram_tensor("yscr", [KC, 128, NT], BF16, kind="ExternalOutput").ap()

    def sb(name, shape, dt=F32):
        return stack.enter_context(nc.sbuf_tensor("sb_" + name, list(shape), dt))

    AW = 12832
    BIG = sb("BIG", [128, KC, NT], BF16)
    RING = sb("RING", [128, 4, 8192], BF16)
    ARENA = sb("ARENA", [128, AW], F32)
    ident = sb("ident", [128, 128])
    pv = sb("pv", [128, NV])
    dv = sb("dv", [128, 160])
    flag = sb("flag", [128, 1])
    cst = sb("cst", [128, 8])
    wsT = sb("wsT", [128, 16, 128], BF16)
    wrb = sb("wrb", [128, 16, 128], BF16)
    wib = sb("wib", [128, 16, 128], BF16)
    onesb = sb("onesb", [128, 128], BF16)
    WSS = sb("wss", [128, 4, 4, 128], BF16)
    bhi = sb("bhi", [1, 2048], BF16)
    blo = sb("blo", [1, 2048], BF16)
    hc = sb("hc", [128, 16])
    xr3 = sb("xr3", [128, 16, 3])
    st = sb("st", [128, 256])
    PS = [stack.enter_context(nc.psum_tensor("ps%d" % i, [128, 512], F32)) for i in range(8)]

    HT = BIG[:, :, 0:512]
    YT = BIG[:, :, 512:1024]

    def A(off, n, dt=F32):
        if dt == F32:
            assert off + n <= AW
            return T(ARENA[:, off:off + n], ("ar", off, off + n))
        assert n % 2 == 0 and off + n // 2 <= AW
        return T(ARENA[:, off:off + n // 2].bitcast(BF16), ("ar", off, off + n // 2))

    def slot_view(s, kind, nb=None):
        flat = RING[:, s, :]
        if kind == "k32n256":
            return flat.rearrange("p (k n) -> p k n", n=256)
        if kind == "k16n512":
            return flat.rearrange("p (k n) -> p k n", n=512)
        if kind == "fo":
            return flat[:, 0:nb * 512].rearrange("p (k n) -> p k n", n=512)
        raise ValueError(kind)

    ring_ctr = [0]

    def ring_load(src_ap, kind, nb=None):
        s = ring_ctr[0] % 4
        ring_ctr[0] += 1
        view = slot_view(s, kind, nb)
        S.op("pool", lambda e, v=view, a=src_ap: e.dma_start(out=v, in_=a),
             w=[("ring", s)], dma=("ring", s))
        return s, view

    PV_GPRE, PV_GFFN, PV_CW, PV_BC, PV_BR, PV_BI, PV_LAM, PV_GA, PV_GB = 0, 32, 64, 128, 144, 160, 176, 192, 208
    DV_HBR, DV_HBI, DV_CL, DV_HCL, DV_QGA, DV_IQGA, DV_HGB, DV_IHGB, DV_T0, DV_T1 = 0, 16, 32, 48, 64, 80, 96, 112, 128, 144
    C_EPS, C_EPS4, C_ONE, C_ZERO, C_Q = 0, 1, 2, 3, 4

    S.group_keys.add("const")
    S.group_keys.add("constp")

    def cload(dst, src, res):
        S.op("sp", lambda e: e.dma_start(out=dst, in_=src), w=[res], dma="const")

    WST = A(0, 2048)
    TRI = A(2048, 128)
    BSP = A(2176, 2048)
    BT1 = A(4224, 2048)
    cload(ident[:, :], ident_d, "ident")
    cload(TRI.ap, tri_d, TRI.r)
    cload(pv[:, :], pv_d, "pv")
    cload(flag[:, :], flag_d, "flag")
    cload(BSP.ap[0:1, :], bsp_d, BSP.r)
    cload(WST.ap.rearrange("p (h t) -> p h t", t=128), wsT_d, WST.r)
    S.op("pool", lambda e: e.dma_start(out=wrb[:, :, :], in_=wr_d), w=["wrb"], dma="constp")
    S.op("pool", lambda e: e.dma_start(out=wib[:, :, :], in_=wi_d), w=["wib"], dma="constp")

    S.op("dve", lambda e: e.memset(cst[:, C_EPS:C_EPS + 1], EPS), w=["cst"])
    S.op("dve", lambda e: e.memset(cst[:, C_EPS4:C_EPS4 + 1], 4 * EPS), w=["cst"])
    S.op("dve", lambda e: e.memset(cst[:, C_ONE:C_ONE + 1], 1.0), w=["cst"])
    S.op("dve", lambda e: e.memset(cst[:, C_ZERO:C_ZERO + 1], 0.0), w=["cst"])
    S.op("dve", lambda e: e.memset(cst[:, C_Q:C_Q + 1], 0.25), w=["cst"])
    S.op("dve", lambda e: e.memset(onesb[:, :], 1.0), w=["onesb"])
    S.op("dve", lambda e: e.memset(hc[:, :], 0.0), w=["hc"])
    S.op("dve", lambda e: e.memset(xr3[:, :, :], 0.0), w=["xr3"])
    S.op("dve", lambda e: e.tensor_tensor(
        out=wsT[:, :, :], in0=WST.ap.rearrange("p (h t) -> p h t", t=128),
        in1=TRI.ap.unsqueeze(1).to_broadcast([128, 16, 128]), op=ALU.mult),
        r=[WST.r, TRI.r], w=["wsT"])
    b0 = BSP.ap[0:1, :]
    b1 = BT1.ap[0:1, :]
    S.op("dve", lambda e: e.tensor_copy(out=bhi[:, :], in_=b0), r=[BSP.r], w=["bhi"])
    S.op("dve", lambda e: e.tensor_copy(out=b1, in_=bhi[:, :]), r=["bhi"], w=[BT1.r])
    S.op("dve", lambda e: e.tensor_tensor(out=b1, in0=b0, in1=b1, op=ALU.subtract), r=[BSP.r, BT1.r], w=[BT1.r])
    S.op("dve", lambda e: e.tensor_copy(out=blo[:, :], in_=b1), r=[BT1.r], w=["blo"])

    def dvs(c):
        return dv[:, c:c + 16]

    def pvs(c):
        return pv[:, c:c + 16]

    S.op("dve", lambda e: e.tensor_scalar(out=dvs(DV_HBR), in0=pvs(PV_BR), scalar1=0.5, scalar2=None, op0=ALU.mult),
         r=["pv"], w=["dv_hbr"])
    S.op("dve", lambda e: e.tensor_scalar(out=dvs(DV_HBI), in0=pvs(PV_BI), scalar1=0.5, scalar2=None, op0=ALU.mult),
         r=["pv"], w=["dv_hbi"])
    S.op("dve", lambda e: e.tensor_scalar(out=dvs(DV_QGA), in0=pvs(PV_GA), scalar1=0.5, scalar2=None, op0=ALU.mult),
         r=["pv"], w=["dv_qga"])
    S.op("dve", lambda e: e.reciprocal(out=dvs(DV_IQGA), in_=dvs(DV_QGA)), r=["dv_qga"], w=["dv_iqga"])
    S.op("dve", lambda e: e.tensor_scalar(out=dvs(DV_HGB), in0=pvs(PV_GB), scalar1=0.5, scalar2=None, op0=ALU.mult),
         r=["pv"], w=["dv_hgb"])
    S.op("dve", lambda e: e.reciprocal(out=dvs(DV_IHGB), in_=dvs(DV_HGB)), r=["dv_hgb"], w=["dv_ihgb"])
    S.op("dve", lambda e: e.tensor_scalar(out=dvs(DV_T0), in0=pvs(PV_LAM), scalar1=-1.0, scalar2=None, op0=ALU.mult),
         r=["pv"], w=["dv_t0"])
    S.op("dve", lambda e: e.tensor_tensor(out=dvs(DV_T0), in0=dvs(DV_T0), in1=pvs(PV_LAM), op=ALU.max),
         r=["pv", "dv_t0"], w=["dv_t0"])
    S.op("act", lambda e: e.activation(out=dvs(DV_T0), in_=dvs(DV_T0), func=AF.Exp, scale=-1.0),
         r=["dv_t0"], w=["dv_t0"])
    S.op("act", lambda e: e.activation(out=dvs(DV_T0), in_=dvs(DV_T0), func=AF.Ln, bias=cst[:, C_ONE:C_ONE + 1], scale=1.0),
         r=["dv_t0", "cst"], w=["dv_t0"])
    S.op("dve", lambda e: e.tensor_scalar(out=dvs(DV_T1), in0=pvs(PV_LAM), scalar1=-1.0, scalar2=0.0, op0=ALU.mult, op1=ALU.max),
         r=["pv"], w=["dv_t1"])
    S.op("dve", lambda e: e.tensor_tensor(out=dvs(DV_T0), in0=dvs(DV_T0), in1=dvs(DV_T1), op=ALU.add),
         r=["dv_t0", "dv_t1"], w=["dv_t0"])
    S.op("dve", lambda e: e.tensor_scalar(out=dvs(DV_CL), in0=dvs(DV_T0), scalar1=-8.0, scalar2=None, op0=ALU.mult),
         r=["dv_t0"], w=["dv_cl"])
    S.op("dve", lambda e: e.tensor_scalar(out=dvs(DV_HCL), in0=dvs(DV_T0), scalar1=-4.0, scalar2=None, op0=ALU.mult),
         r=["dv_t0"], w=["dv_hcl"])

    def rstd_small(dst, src, scale, eps_col, r, w):
        S.op("act", lambda e: e.activation(out=dst, in_=src, func=AF.Sqrt, bias=cst[:, eps_col:eps_col + 1], scale=scale),
             r=list(r) + ["cst"], w=list(w))
        S.op("dve", lambda e: e.reciprocal(out=dst, in_=dst), r=list(w), w=list(w))

    tp_ctr = [0]

    def transpose_to_fm(xn, gcol, dest_fn, dest_res, src_res=None):
        for q in range(8):
            b = tp_ctr[0] % 8
            tp_ctr[0] += 1
            bank = PS[b]

            def mm(e, q=q, bank=bank):
                ins = None
                for j in range(4):
                    kc = q * 4 + j
                    ins = e.transpose(bank[:, j * 128:(j + 1) * 128], xn.ap[:, kc * 128:(kc + 1) * 128], ident[:, :])
                return ins
            S.op("pe", mm, r=[xn.r if src_res is None else src_res(q), "ident"], w=[("ps", b)])
            if q in (3, 7):
                def ev(e, q=q, bank=bank):
                    ins = None
                    for j in range(4):
                        kc = q * 4 + j
                        ins = e.activation(out=dest_fn(q)[:, j, :], in_=bank[:, j * 128:(j + 1) * 128], func=AF.Identity,
                                           scale=pv[:, gcol + kc:gcol + kc + 1])
                    return ins
                S.op("act", ev, r=[("ps", b), "pv"], w=[dest_res])
                continue
            S.op("dve", lambda e, q=q, bank=bank: e.tensor_tensor(
                out=dest_fn(q), in0=bank[:, :].rearrange("p (j t) -> p j t", t=128),
                in1=pv[:, gcol + q * 4:gcol + q * 4 + 4].unsqueeze(2).to_broadcast([128, 4, 128]), op=ALU.mult),
                r=[("ps", b), "pv"], w=[dest_res])

    XT = [A(0, 4096), A(4096, 4096)]
    JUNK = A(8192, 4096, BF16)

    def phase1(x_ap):
        for tg in range(8):
            xt = XT[tg % 2]
            r0 = tg * 128
            c = 208 + 2 * (tg % 2)
            S.op("sp", lambda e, xt=xt, r0=r0: e.dma_start(out=xt.ap, in_=x_ap[r0:r0 + 128, :]),
                 w=[xt.r], dma=("xt", tg % 2))
            S.op("act", lambda e, xt=xt, c=c: e.activation(out=JUNK.ap, in_=xt.ap, func=AF.Square, accum_out=st[:, c:c + 1]),
                 r=[xt.r], w=[JUNK.r, ("p1s", tg % 2)])
            rstd_small(st[:, c + 1:c + 2], st[:, c:c + 1], 1.0 / D, C_EPS, [("p1s", tg % 2)], [("p1r", tg % 2)])
            lo_r = ("ar", xt.r[1], xt.r[1] + 2048)
            hi_r = ("ar", xt.r[1] + 2048, xt.r[2])
            S.op("act", lambda e, xt=xt, c=c: e.activation(out=xt.ap[:, 0:2048], in_=xt.ap[:, 0:2048], func=AF.Identity,
                                                           scale=st[:, c + 1:c + 2]),
                 r=[lo_r, ("p1r", tg % 2)], w=[lo_r])
            S.op("dve", lambda e, xt=xt, c=c: e.tensor_scalar(out=xt.ap[:, 2048:4096], in0=xt.ap[:, 2048:4096],
                                                              scalar1=st[:, c + 1:c + 2], scalar2=None, op0=ALU.mult),
                 r=[hi_r, ("p1r", tg % 2)], w=[hi_r])
            transpose_to_fm(xt, PV_GPRE, lambda q, tg=tg: BIG[:, q * 4:q * 4 + 4, tg * 128:(tg + 1) * 128], "HT",
                            src_res=lambda q, lo_r=lo_r, hi_r=hi_r: lo_r if q < 4 else hi_r)

    GVB = A(0, 2048)
    o_ = 2048
    XRB = A(o_, 516); o_ += 516
    XC = [A(o_, 512), A(o_ + 512, 512)]; o_ += 1024
    GG = [A(o_, 512), A(o_ + 512, 512)]; o_ += 1024
    XCB = [A(o_, 512, BF16), A(o_ + 256, 512, BF16)]; o_ += 512
    YSQ = [A(o_, 512, BF16), A(o_ + 256, 512, BF16)]; o_ += 512
    YST = [A(o_, 512, BF16), A(o_ + 256, 512, BF16)]; o_ += 512
    TMP = A(o_, 512); o_ += 512
    TR = A(o_, 512); o_ += 512
    TI = A(o_, 512); o_ += 512
    AA = A(o_, 512); o_ += 512
    A2 = A(o_, 512); o_ += 512
    HH = A(o_, 512); o_ += 512
    VN = A(o_, 2048, BF16); o_ += 1024
    GU = [A(o_ + i * 512, 512) for i in range(4)]; o_ += 2048
    GV = A(o_, 512); o_ += 512
    assert o_ <= AW, o_
    SSQ = PS[7]

    def y_store(yst, kc, half):
        S.op("sp", lambda e: e.dma_start(out=yscr[kc, :, half * 512:(half + 1) * 512], in_=yst.ap),
             r=[yst.r], w=[("yscr", kc)], dma=("yst", yst.r[1]))

    def gelu2(dst, src_ps, src_res):
        S.op("act", lambda e: e.activation(out=TMP.ap, in_=src_ps, func=AF.Square, scale=SQC3), r=[src_res], w=[TMP.r])
        S.op("dve", lambda e: e.scalar_tensor_tensor(out=TMP.ap, in0=TMP.ap, scalar=C1, in1=src_ps, op0=ALU.add, op1=ALU.mult),
             r=[TMP.r, src_res], w=[TMP.r])
        S.op("act", lambda e: e.activation(out=TMP.ap, in_=TMP.ap, func=AF.Tanh), r=[TMP.r], w=[TMP.r])
        S.op("dve", lambda e: e.scalar_tensor_tensor(out=dst.ap, in0=TMP.ap, scalar=1.0, in1=src_ps, op0=ALU.add, op1=ALU.mult),
             r=[TMP.r, src_res], w=[dst.r])

    def proj_fm(bank_i, slot, view, jj, half):
        bank = PS[bank_i]

        def mm(e):
            ins = None
            for kc in range(KC):
                ins = e.matmul(bank[:, :], lhsT=view[:, kc, jj * 128:(jj + 1) * 128],
                               rhs=BIG[:, kc, half * 512:(half + 1) * 512],
                               start=(kc == 0), stop=(kc == KC - 1))
            return ins
        S.op("pe", mm, r=[("ring", slot), "HT"], w=[("ps", bank_i)])

    def ssq_mm(ysq, colbase, half):
        def mm(e):
            ins = None
            for t in range(4):
                c = (half * 4 + t) * 32 + colbase
                ins = e.matmul(SSQ[:, c:c + 1], lhsT=ysq.ap[:, t * 128:(t + 1) * 128], rhs=onesb[:, 0:1],
                               start=True, stop=True)
            return ins
        S.op("pe", mm, r=[ysq.r, "onesb"], w=[("ps", 7)])

    def gmlp():
        pend = []

        def flush(keep_res=None):
            nonlocal pend
            rest = []
            for (yq, col, hf_) in pend:
                if keep_res is not None and yq.r != keep_res:
                    rest.append((yq, col, hf_))
                else:
                    ssq_mm(yq, col, hf_)
            pend = rest

        SSV = st[:, 216:232]
        k_ = 0
        for hg in range(4):
            sA, vA = ring_load(w_in[0:2048, DA + hg * 512:DA + (hg + 1) * 512].rearrange("(k p) n -> p k n", p=128), "k16n512")
            sB, vB = ring_load(w_in[2048:4096, DA + hg * 512:DA + (hg + 1) * 512].rearrange("(k p) n -> p k n", p=128), "k16n512")
            uslots = []
            for j in range(2):
                c0 = hg * 512 + j * 256
                uslots.append(ring_load(w_in[:, c0:c0 + 256].rearrange("(k p) n -> p k n", p=128), "k32n256"))
            for half in range(2):
                for t in range(4):
                    tg = half * 4 + t

                    def mm(e, t=t, tg=tg, vA=vA, vB=vB):
                        ins = None
                        for k in range(KC):
                            vw = vA if k < 16 else vB
                            ins = e.matmul(PS[t][:, :], lhsT=BIG[:, k, tg * 128:(tg + 1) * 128], rhs=vw[:, k % 16, :],
                                           start=(k == 0), stop=(k == KC - 1))
                        return ins
                    S.op("pe", mm, r=[("ring", sA), ("ring", sB), "HT"], w=[("ps", t)])
                    if t == 0:
                        flush()
                for t in range(4):
                    gelu2(GV, PS[t][:, :], ("ps", t))
                    S.op("act", lambda e: e.activation(out=TMP.ap, in_=GV.ap, func=AF.Square), r=[GV.r], w=[TMP.r])
                    S.op("dve", lambda e, t=t: e.tensor_reduce(out=SSV[:, t * 4:(t + 1) * 4],
                                                               in_=TMP.ap.rearrange("p (h d) -> p h d", d=128),
                                                               axis=AX.X, op=ALU.add), r=[TMP.r], w=["ssv"])
                    S.op("dve", lambda e, t=t, hg=hg: e.tensor_tensor(
                        out=VN.ap[:, t * 512:(t + 1) * 512], in0=GV.ap, in1=GVB.ap[:, hg * 512:(hg + 1) * 512], op=ALU.mult),
                        r=[GV.r, GVB.r], w=[VN.r])
                rstd_small(SSV, SSV, 1.0 / 128, C_EPS4, ["ssv"], ["ssv"])
                for t in range(4):
                    S.op("dve", lambda e, t=t, hg=hg: e.tensor_tensor(
                        out=WSS[:, t, :, :], in0=wsT[:, hg * 4:(hg + 1) * 4, :],
                        in1=SSV[:, t * 4:(t + 1) * 4].unsqueeze(2).to_broadcast([128, 4, 128]), op=ALU.mult),
                        r=["wsT", "ssv"], w=["WSS"])
                for hl in range(4):
                    sl, vw = uslots[hl // 2]
                    ub = 4 + (hl % 2)
                    proj_fm(ub, sl, vw, hl % 2, half)
                    gelu2(GU[hl], PS[ub][:, :], ("ps", ub))
                for hl in range(4):
                    h = hg * 4 + hl

                    def mm(e, hl=hl, h=h):
                        ins = None
                        for t in range(4):
                            o = PS[hl][:, t * 128:(t + 1) * 128]
                            e.matmul(o, lhsT=VN.ap[:, t * 512 + hl * 128:t * 512 + (hl + 1) * 128], rhs=WSS[:, t, hl, :],
                                     start=True, stop=False)
                            e.matmul(o, lhsT=onesb[0:1, :], rhs=bhi[0:1, h * 128:(h + 1) * 128], start=False, stop=False)
                            ins = e.matmul(o, lhsT=onesb[0:1, :], rhs=blo[0:1, h * 128:(h + 1) * 128], start=False, stop=True)
                        return ins
                    S.op("pe", mm, r=[VN.r, "WSS", "onesb", "bhi", "blo"], w=[("ps", hl)])
                    yst = YST[k_ % 2]
                    yq = YSQ[k_ % 2]
                    k_ += 1
                    S.op("dve", lambda e, h=h, hl=hl, yst=yst: e.scalar_tensor_tensor(
                        out=yst.ap, in0=PS[hl][:, :], scalar=dv[:, DV_QGA + h:DV_QGA + h + 1], in1=GU[hl].ap,
                        op0=ALU.mult, op1=ALU.mult), r=[("ps", hl), GU[hl].r, "dv_qga"], w=[yst.r])
                    y_store(yst, h, half)
                    flush(keep_res=yq.r)
                    S.op("act", lambda e, h=h, yq=yq, yst=yst: e.activation(out=yq.ap, in_=yst.ap, func=AF.Square,
                                                                          scale=dv[:, DV_IQGA + h:DV_IQGA + h + 1]),
                         r=[yst.r, "dv_iqga"], w=[yq.r])
                    pend.append((yq, h, half))
        flush()

    def rglru(main):
        slots = {}
        NS = 32
        GB = [2, 3, 6]

        def pe0(s):
            n, half = s // 2, s % 2
            if s % 4 == 0:
                c0 = DA * 2 + DB + n * 128
                slots["x"] = ring_load(w_in[:, c0:c0 + 256].rearrange("(k p) n -> p k n", p=128), "k32n256")
                if main:
                    c0 = DA * 2 + n * 128
                    slots["g"] = ring_load(w_in[:, c0:c0 + 256].rearrange("(k p) n -> p k n", p=128), "k32n256")
            if main:
                proj_fm(GB[s % 3], slots["g"][0], slots["g"][1], n % 2, half)
            proj_fm(s % 2, slots["x"][0], slots["x"][1], n % 2, half)

        def a1(s):
            n, par = s // 2, s % 2
            xps = PS[par][:, :]
            xc = XC[par]
            S.op("act", lambda e: e.activation(out=XRB.ap[:, 3:515], in_=xps, func=AF.Copy), r=[("ps", par)], w=[XRB.r])
            S.op("act", lambda e: e.activation(out=xc.ap, in_=xps, func=AF.Identity,
                                               bias=pv[:, PV_BC + n:PV_BC + n + 1],
                                               scale=pv[:, PV_CW + n * 4 + 3:PV_CW + n * 4 + 4]),
                 r=[("ps", par), "pv"], w=[xc.r])
            if main:
                gb = GB[s % 3]
                S.op("act", lambda e: e.activation(out=GG[par].ap, in_=PS[gb][:, :], func=AF.Square, scale=SQC3),
                     r=[("ps", gb)], w=[GG[par].r])

        def d1(s):
            n, par = s // 2, s % 2
            xc = XC[par]
            S.op("dve", lambda e: e.tensor_copy(out=XRB.ap[:, 0:3], in_=xr3[:, n, :]), r=["xr3"], w=[XRB.r])
            for k in (2, 1, 0):
                S.op("dve", lambda e, k=k: e.scalar_tensor_tensor(
                    out=xc.ap, in0=XRB.ap[:, k:k + 512], scalar=pv[:, PV_CW + n * 4 + k:PV_CW + n * 4 + k + 1],
                    in1=xc.ap, op0=ALU.mult, op1=ALU.add), r=[XRB.r, xc.r, "pv"], w=[xc.r])
            S.op("dve", lambda e: e.tensor_copy(out=xr3[:, n, :], in_=XRB.ap[:, 512:515]), r=[XRB.r], w=["xr3"])
            S.op("dve", lambda e: e.tensor_copy(out=XCB[par].ap, in_=xc.ap), r=[xc.r], w=[XCB[par].r])
            if main:
                gb = GB[s % 3]
                S.op("dve", lambda e: e.scalar_tensor_tensor(out=GG[par].ap, in0=GG[par].ap, scalar=C1, in1=PS[gb][:, :],
                                                             op0=ALU.add, op1=ALU.mult),
                     r=[GG[par].r, ("ps", gb)], w=[GG[par].r])

        def pe1(s):
            n, par = s // 2, s % 2

            def mm(e):
                e.matmul(PS[4][:, :], lhsT=wrb[:, n, :], rhs=XCB[par].ap, start=True, stop=True)
                return e.matmul(PS[5][:, :], lhsT=wib[:, n, :], rhs=XCB[par].ap, start=True, stop=True)
            S.op("pe", mm, r=["wrb", "wib", XCB[par].r], w=[("ps", 4), ("ps", 5)])

        def a2(s):
            n, par = s // 2, s % 2
            if main:
                S.op("act", lambda e: e.activation(out=GG[par].ap, in_=GG[par].ap, func=AF.Tanh), r=[GG[par].r], w=[GG[par].r])
            S.op("act", lambda e: e.activation(out=TR.ap, in_=PS[4][:, :], func=AF.Tanh,
                                               bias=dv[:, DV_HBR + n:DV_HBR + n + 1], scale=0.5),
                 r=[("ps", 4), "dv_hbr"], w=[TR.r])
            S.op("act", lambda e: e.activation(out=TI.ap, in_=PS[5][:, :], func=AF.Tanh,
                                               bias=dv[:, DV_HBI + n:DV_HBI + n + 1], scale=0.5),
                 r=[("ps", 5), "dv_hbi"], w=[TI.r])
            S.op("act", lambda e: e.activation(out=AA.ap, in_=TR.ap, func=AF.Exp, bias=dv[:, DV_HCL + n:DV_HCL + n + 1],
                                               scale=dv[:, DV_HCL + n:DV_HCL + n + 1]), r=[TR.r, "dv_hcl"], w=[AA.r])
            S.op("act", lambda e: e.activation(out=A2.ap, in_=TR.ap, func=AF.Exp, bias=dv[:, DV_CL + n:DV_CL + n + 1],
                                               scale=dv[:, DV_CL + n:DV_CL + n + 1]), r=[TR.r, "dv_cl"], w=[A2.r])
            S.op("act", lambda e: e.activation(out=A2.ap, in_=A2.ap, func=AF.Sqrt, bias=cst[:, C_Q:C_Q + 1], scale=-0.25),
                 r=[A2.r, "cst"], w=[A2.r])

        def d2(s):
            n, half, par = s // 2, s % 2, s % 2
            xc = XC[par]
            if main:
                gb = GB[s % 3]
                S.op("dve", lambda e: e.scalar_tensor_tensor(out=GG[par].ap, in0=GG[par].ap, scalar=1.0, in1=PS[gb][:, :],
                                                             op0=ALU.add, op1=ALU.mult),
                     r=[GG[par].r, ("ps", gb)], w=[GG[par].r])
            S.op("dve", lambda e: e.scalar_tensor_tensor(out=TI.ap, in0=TI.ap, scalar=1.0, in1=xc.ap,
                                                         op0=ALU.add, op1=ALU.mult), r=[TI.r, xc.r], w=[TI.r])
            S.op("dve", lambda e: e.scalar_tensor_tensor(out=TI.ap, in0=A2.ap, scalar=0.5e-6, in1=TI.ap,
                                                         op0=ALU.max, op1=ALU.mult), r=[TI.r, A2.r], w=[TI.r])
            S.op("dve", lambda e: e.tensor_tensor_scan(out=HH.ap, data0=AA.ap, data1=TI.ap, initial=hc[:, n:n + 1],
                                                       op0=ALU.mult, op1=ALU.add), r=[AA.r, TI.r, "hc"], w=[HH.r])
            S.op("dve", lambda e: e.tensor_copy(out=hc[:, n:n + 1], in_=HH.ap[:, 511:512]), r=[HH.r], w=["hc"])
            if main:
                yst = YST[par]
                S.op("dve", lambda e: e.scalar_tensor_tensor(
                    out=yst.ap, in0=HH.ap, scalar=dv[:, DV_HGB + n:DV_HGB + n + 1], in1=GG[par].ap,
                    op0=ALU.mult, op1=ALU.mult), r=[HH.r, GG[par].r, "dv_hgb"], w=[yst.r])
                y_store(yst, 16 + n, half)

        def a3(s):
            n, par = s // 2, s % 2
            S.op("act", lambda e: e.activation(out=YSQ[par].ap, in_=YST[par].ap, func=AF.Square,
                                               scale=dv[:, DV_IHGB + n:DV_IHGB + n + 1]),
                 r=[YST[par].r, "dv_ihgb"], w=[YSQ[par].r])

        for k in range(NS + 3):
            if k < NS:
                pe0(k)
                a1(k)
                d1(k)
            if 1 <= k <= NS:
                pe1(k - 1)
                a2(k - 1)
                d2(k - 1)
            if main and 2 <= k <= NS + 1:
                a3(k - 2)
            if main and 3 <= k <= NS + 2:
                s = k - 3
                ssq_mm(YSQ[s % 2], 16 + s // 2, s % 2)

    OB = [A(2048 + i * 512, 512) for i in range(3)]
    TMPO = [A(3584 + i * 512, 512) for i in range(2)]
    SSO = st[:, 64:128]
    RA = st[:, 16:24]
    RB = st[:, 40:48]

    def phase3a():
        S.group_keys.add("yld")
        for kc in range(KC):
            S.inherit(("YT", kc), ["HT"])
            S.op("sp", lambda e, kc=kc: e.dma_start(out=BIG[:, kc, :], in_=yscr[kc, :, :]),
                 r=[("yscr", kc)], w=[("YT", kc)], dma="yld")
        S.op("dve", lambda e: e.tensor_reduce(out=RA, in_=SSQ[:, 0:256].rearrange("p (t c) -> p t c", c=32)[:, :, 0:16],
                                              axis=AX.X, op=ALU.add), r=[("ps", 7)], w=["RA"])
        S.op("dve", lambda e: e.tensor_reduce(out=RB, in_=SSQ[:, 0:256].rearrange("p (t c) -> p t c", c=32)[:, :, 16:32],
                                              axis=AX.X, op=ALU.add), r=[("ps", 7)], w=["RB"])
        rstd_small(RA, RA, 1.0 / DA, C_EPS, ["RA"], ["RA"])
        rstd_small(RB, RB, 1.0 / DB, C_EPS, ["RB"], ["RB"])
        i_ = 0
        for c in range(8):
            sa, va = ring_load(w_out[0:2048, c * 512:(c + 1) * 512].rearrange("(k p) n -> p k n", p=128), "k16n512")
            sb_, vb = ring_load(w_out[2048:4096, c * 512:(c + 1) * 512].rearrange("(k p) n -> p k n", p=128), "k16n512")
            for tg in range(8):
                ba = tg % 2
                bb = 2 + tg % 2

                def mm(e, va=va, vb=vb, tg=tg, ba=ba, bb=bb):
                    ins = None
                    for k in range(16):
                        e.matmul(PS[ba][:, :], lhsT=BIG[:, k, tg * 128:(tg + 1) * 128], rhs=va[:, k, :],
                                 start=(k == 0), stop=(k == 15))
                    for k in range(16):
                        ins = e.matmul(PS[bb][:, :], lhsT=BIG[:, 16 + k, tg * 128:(tg + 1) * 128], rhs=vb[:, k, :],
                                       start=(k == 0), stop=(k == 15))
                    return ins
                S.op("pe", mm, r=[("ring", sa), ("ring", sb_)] + [("YT", k) for k in range(32)],
                     w=[("ps", ba), ("ps", bb)])
                tm = TMPO[i_ % 2]
                ob = OB[i_ % 3]
                S.op("act", lambda e, tm=tm, ba=ba, tg=tg: e.activation(out=tm.ap, in_=PS[ba][:, :], func=AF.Identity,
                                                                        scale=RA[:, tg:tg + 1]),
                     r=[("ps", ba), "RA"], w=[tm.r])
                S.op("dve", lambda e, tm=tm, ob=ob, bb=bb, tg=tg: e.scalar_tensor_tensor(
                    out=ob.ap, in0=PS[bb][:, :], scalar=RB[:, tg:tg + 1], in1=tm.ap, op0=ALU.mult, op1=ALU.add),
                    r=[("ps", bb), "RB", tm.r], w=[ob.r])
                S.op("act", lambda e, tm=tm, ob=ob, tg=tg, c=c: e.activation(
                    out=tm.ap, in_=ob.ap, func=AF.Square, accum_out=SSO[:, tg * 8 + c:tg * 8 + c + 1]),
                    r=[ob.r], w=[tm.r, ("SSO", tg)])
                S.op("sp", lambda e, ob=ob, tg=tg, c=c: e.dma_start(
                    out=oscr[tg * 128:(tg + 1) * 128, c * 512:(c + 1) * 512], in_=ob.ap),
                    r=[ob.r], w=[("oscr", tg)], dma=("obw", i_ % 3))
                i_ += 1

    OT = [A(0, 4096), T(RING[:, 0, :].bitcast(F32), ("ring", 0)), T(RING[:, 2, :].bitcast(F32), ("ring", 2))]
    XT3 = [A(4096, 4096), T(RING[:, 1, :].bitcast(F32), ("ring", 1)), T(RING[:, 3, :].bitcast(F32), ("ring", 3))]
    GP = A(8192, 4096)
    RO = st[:, 24:32]
    RF = st[:, 32:40]
    SSF = st[:, 128:192]

    def phase3b():
        S.op("sp", lambda e: e.dma_start(out=GP.ap, in_=gpm_d), w=[GP.r], dma="gp")

        def loads(tg):
            p = tg % 3
            S.op("sp", lambda e: e.dma_start(out=OT[p].ap, in_=oscr[tg * 128:(tg + 1) * 128, :]),
                 r=[("oscr", tg)], w=[OT[p].r], dma=("otl", p))
            S.op("sp", lambda e: e.dma_start(out=XT3[p].ap, in_=x_own[tg * 128:(tg + 1) * 128, :]),
                 w=[XT3[p].r], dma=("xt3l", p))

        def d1(tg):
            p = tg % 3
            ot, xt3 = OT[p], XT3[p]
            S.op("dve", lambda e: e.tensor_reduce(out=RO[:, tg:tg + 1], in_=SSO[:, tg * 8:(tg + 1) * 8],
                                                  axis=AX.X, op=ALU.add), r=[("SSO", tg)], w=[("RO", tg)])
            rstd_small(RO[:, tg:tg + 1], RO[:, tg:tg + 1], 1.0 / D, C_EPS, [("RO", tg)], [("RO", tg)])
            S.op("dve", lambda e: e.scalar_tensor_tensor(out=ot.ap, in0=ot.ap, scalar=RO[:, tg:tg + 1], in1=GP.ap,
                                                         op0=ALU.mult, op1=ALU.mult),
                 r=[ot.r, ("RO", tg), GP.r], w=[ot.r])
            S.op("dve", lambda e: e.tensor_tensor(out=ot.ap, in0=ot.ap, in1=xt3.ap, op=ALU.add),
                 r=[ot.r, xt3.r], w=[ot.r])
            S.op("sp", lambda e: e.dma_start(out=out_d[tg * 128:(tg + 1) * 128, :], in_=ot.ap),
                 r=[ot.r], w=[("outd", tg)], dma=("x1w", p))

        def a_(tg):
            p = tg % 3
            ot, xt3 = OT[p], XT3[p]
            c2 = 200 + 2 * p
            S.op("act", lambda e: e.activation(out=xt3.ap.bitcast(BF16)[:, 0:4096], in_=ot.ap,
                                               func=AF.Square, accum_out=st[:, c2:c2 + 1]),
                 r=[ot.r], w=[xt3.r, ("st2", p)])
            rstd_small(st[:, c2 + 1:c2 + 2], st[:, c2:c2 + 1], 1.0 / D, C_EPS, [("st2", p)], [("st3", p)])
            S.op("act", lambda e: e.activation(out=xt3.ap, in_=ot.ap, func=AF.Identity, scale=st[:, c2 + 1:c2 + 2]),
                 r=[ot.r, ("st3", p)], w=[xt3.r])

        loads(0)
        loads(1)
        loads(2)
        d1(0)
        a_(0)
        for tg in range(8):
            if tg + 1 < 8:
                d1(tg + 1)
            transpose_to_fm(XT3[tg % 3], PV_GFFN, lambda q, tg=tg: BIG[:, q * 4:q * 4 + 4, tg * 128:(tg + 1) * 128], "H2T")
            if tg + 3 < 8:
                loads(tg + 3)
            if tg + 1 < 8:
                a_(tg + 1)

    ACTG_T = A(0, 7168)
    ACTG = ACTG_T.ap.bitcast(BF16).rearrange("p (k t) -> p k t", t=1024)
    SG = [A(7168, 512), A(7680, 512)]
    NFIN = 5
    FIN = [A(8192 + i * 512, 512) for i in range(NFIN)]
    FS = [A(10752 + i * 512, 512) for i in range(3)]
    SQJ = A(12288, 512)

    def ffn():
        groups = []
        j = 0
        for nb in [12] * 6 + [14]:
            groups.append((j, nb))
            j += nb
        assert j == NFB
        ti = [0]
        for gi, (j0, nb) in enumerate(groups):
            last = gi == len(groups) - 1
            for pj in range(nb // 2):
                ja = j0 + pj * 2
                sg_, vg = ring_load(w_fi[:, ja * 128:ja * 128 + 256].rearrange("(k p) n -> p k n", p=128), "k32n256")
                su_, vu = ring_load(w_fi[:, DFF + ja * 128:DFF + ja * 128 + 256].rearrange("(k p) n -> p k n", p=128), "k32n256")
                for jj in range(2):
                    jl = pj * 2 + jj
                    for hf in range(2):
                        for (bank_i, sl, vw) in ((hf, sg_, vg), (2 + hf, su_, vu)):
                            def mm(e, bank_i=bank_i, vw=vw, jj=jj, hf=hf):
                                ins = None
                                for kc in range(KC):
                                    ins = e.matmul(PS[bank_i][:, :], lhsT=vw[:, kc, jj * 128:(jj + 1) * 128],
                                                   rhs=BIG[:, kc, hf * 512:(hf + 1) * 512],
                                                   start=(kc == 0), stop=(kc == KC - 1))
                                return ins
                            S.op("pe", mm, r=[("ring", sl), "H2T"], w=[("ps", bank_i)])
                        S.op("act", lambda e, hf=hf: e.activation(out=SG[hf].ap, in_=PS[hf][:, :], func=AF.Silu),
                             r=[("ps", hf)], w=[SG[hf].r])
                        S.op("dve", lambda e, hf=hf, jl=jl: e.tensor_tensor(
                            out=ACTG[:, jl, hf * 512:(hf + 1) * 512], in0=SG[hf].ap, in1=PS[2 + hf][:, :], op=ALU.mult),
                            r=[SG[hf].r, ("ps", 2 + hf)], w=[ACTG_T.r])
            tiles = [(c, tg) for c in range(8) for tg in range(8)]
            if gi > 0:
                for pre in range(NFIN - 1):
                    c, tg = tiles[pre]
                    fb = (ti[0] + pre) % NFIN
                    S.op("sp", lambda e, fb=fb, c=c, tg=tg: e.dma_start(
                        out=FIN[fb].ap, in_=fscr[tg * 128:(tg + 1) * 128, c * 512:(c + 1) * 512]),
                        r=[("fscr", tg, c)], w=[FIN[fb].r], dma=("fin", fb))
            vo = None
            so_ = None
            for idx, (c, tg) in enumerate(tiles):
                if tg == 0:
                    so_, vo = ring_load(w_fo[j0 * 128:(j0 + nb) * 128, c * 512:(c + 1) * 512]
                                        .rearrange("(k p) n -> p k n", p=128), "fo", nb)
                fbk = 4 + idx % 4

                def mm(e, vo=vo, tg=tg, fbk=fbk, nb=nb):
                    ins = None
                    for k in range(nb):
                        ins = e.matmul(PS[fbk][:, :], lhsT=ACTG[:, k, tg * 128:(tg + 1) * 128], rhs=vo[:, k, :],
                                       start=(k == 0), stop=(k == nb - 1))
                    return ins
                S.op("pe", mm, r=[("ring", so_), ACTG_T.r], w=[("ps", fbk)])
                fb = ti[0] % 3
                fbi = ti[0] % NFIN
                if gi > 0 and idx + NFIN - 1 < len(tiles):
                    c2, tg2 = tiles[idx + NFIN - 1]
                    fb2 = (ti[0] + NFIN - 1) % NFIN
                    S.op("sp", lambda e, fb2=fb2, c2=c2, tg2=tg2: e.dma_start(
                        out=FIN[fb2].ap, in_=fscr[tg2 * 128:(tg2 + 1) * 128, c2 * 512:(c2 + 1) * 512]),
                        r=[("fscr", tg2, c2)], w=[FIN[fb2].r], dma=("fin", fb2))
                if gi == 0:
                    S.op("act", lambda e, fb=fb, fbk=fbk: e.activation(out=FS[fb].ap, in_=PS[fbk][:, :], func=AF.Copy),
                         r=[("ps", fbk)], w=[FS[fb].r])
                else:
                    S.op("dve", lambda e, fb=fb, fbk=fbk, fbi=fbi: e.tensor_tensor(out=FS[fb].ap, in0=PS[fbk][:, :],
                                                                                   in1=FIN[fbi].ap, op=ALU.add),
                         r=[("ps", fbk), FIN[fbi].r], w=[FS[fb].r])
                if last:
                    S.op("act", lambda e, fb=fb, tg=tg, c=c: e.activation(
                        out=SQJ.ap, in_=FS[fb].ap, func=AF.Square, accum_out=SSF[:, tg * 8 + c:tg * 8 + c + 1]),
                        r=[FS[fb].r], w=[SQJ.r, ("SSF", tg)])
                S.op("sp", lambda e, fb=fb, c=c, tg=tg: e.dma_start(
                    out=fscr[tg * 128:(tg + 1) * 128, c * 512:(c + 1) * 512], in_=FS[fb].ap),
                    r=[FS[fb].r], w=[("fscr", tg, c)], dma=("fsw", fb))
                ti[0] += 1

    def final():
        OTF = OT
        XTF = XT3
        S.op("sp", lambda e: e.dma_start(out=GP.ap, in_=gpf_d), w=[GP.r], dma="gp")

        def loads(tg):
            p = tg % 3
            S.op("sp", lambda e: e.dma_start(out=OTF[p].ap, in_=fscr[tg * 128:(tg + 1) * 128, :]),
                 r=[("fscr", tg, c) for c in range(8)], w=[OTF[p].r], dma=("otl", p))
            S.op("sp", lambda e: e.dma_start(out=XTF[p].ap, in_=out_d[tg * 128:(tg + 1) * 128, :]),
                 r=[("outd", tg)], w=[XTF[p].r], dma=("xt3l", p))

        loads(0)
        loads(1)
        for tg in range(8):
            p = tg % 3
            ot, xt3 = OTF[p], XTF[p]
            if tg + 2 < 8:
                loads(tg + 2)
            S.op("dve", lambda e, tg=tg: e.tensor_reduce(out=RF[:, tg:tg + 1], in_=SSF[:, tg * 8:(tg + 1) * 8],
                                                         axis=AX.X, op=ALU.add), r=[("SSF", tg)], w=[("RF", tg)])
            rstd_small(RF[:, tg:tg + 1], RF[:, tg:tg + 1], 1.0 / D, C_EPS, [("RF", tg)], [("RF", tg)])
            S.op("dve", lambda e, tg=tg, ot=ot: e.scalar_tensor_tensor(out=ot.ap, in0=ot.ap, scalar=RF[:, tg:tg + 1], in1=GP.ap,
                                                                       op0=ALU.mult, op1=ALU.mult),
                 r=[ot.r, ("RF", tg), GP.r], w=[ot.r])
            S.op("dve", lambda e, ot=ot, xt3=xt3: e.tensor_tensor(out=ot.ap, in0=ot.ap, in1=xt3.ap, op=ALU.add),
                 r=[ot.r, xt3.r], w=[ot.r])
            S.op("sp", lambda e, tg=tg, ot=ot: e.dma_start(out=out_d[tg * 128:(tg + 1) * 128, :], in_=ot.ap),
                 r=[ot.r], w=[("outd", tg)], dma=("x1w", p))

    phase1(x_prev)
    rglru(main=False)
    S.op("dve", lambda e: e.tensor_scalar(out=hc[:, :], in0=hc[:, :], scalar1=flag[:, 0:1], scalar2=None, op0=ALU.mult),
         r=["hc", "flag"], w=["hc"])
    S.op("dve", lambda e: e.tensor_scalar(out=xr3[:, :, :].rearrange("p n k -> p (n k)"),
                                          in0=xr3[:, :, :].rearrange("p n k -> p (n k)"),
                                          scalar1=flag[:, 0:1], scalar2=None, op0=ALU.mult),
         r=["xr3", "flag"], w=["xr3"])
    phase1(x_own)
    S.op("sp", lambda e: e.dma_start(out=GVB.ap, in_=gv_d), w=[GVB.r], dma="gvb")
    gmlp()
    rglru(main=True)
    phase3a()
    S.inherit("H2T", [("YT", k) for k in range(32)])
    phase3b()
    if STOP_AFTER != "3b":
        ring_ctr[0] += (2 - ring_ctr[0]) % 4
        ffn()
        final()
    S.op("sp", None, r=[("outd", tg) for tg in range(8)] + [("oscr", tg) for tg in range(8)]
         + [("fscr", tg, c) for tg in range(8) for c in range(8)] + [("yscr", k) for k in range(32)], w=["done"])
    S.emit(nc, stack)
    stack.close()
    return nc


_NC_CACHE = {}


def _fm(v, n):
    return np.ascontiguousarray(np.asarray(v, np.float32).reshape(n, 128).T)


def kernel(x, pre_mix_g, w_in, gmlp_v_norm_g, w_spatial, b_spatial, w_conv, b_conv, w_r, b_r, w_i, b_i,
           lru_lambda, out_norm_a_g, out_norm_b_g, w_out, post_mix_g, pre_ffn_g, w_ffn_in, w_ffn_out, post_ffn_g):
    f32 = np.float32
    x = np.asarray(x, f32)
    B, SEQ, _ = x.shape
    w_in0 = np.ascontiguousarray(np.asarray(w_in, f32)[0])
    w_out0 = np.ascontiguousarray(np.asarray(w_out, f32)[0])
    w_fi0 = np.ascontiguousarray(np.asarray(w_ffn_in, f32)[0])
    w_fo0 = np.ascontiguousarray(np.asarray(w_ffn_out, f32)[0])
    pv = np.zeros((128, NV), f32)
    pv[:, 0:32] = _fm(pre_mix_g[0], 32)
    pv[:, 32:64] = _fm(pre_ffn_g[0], 32)
    wc = np.asarray(w_conv, f32)[0]
    pv[:, 64:128] = wc.reshape(4, 16, 128).transpose(2, 1, 0).reshape(128, 64)
    pv[:, 128:144] = _fm(b_conv[0], 16)
    pv[:, 144:160] = _fm(b_r[0], 16)
    pv[:, 160:176] = _fm(b_i[0], 16)
    pv[:, 176:192] = _fm(lru_lambda[0], 16)
    pv[:, 192:208] = _fm(out_norm_a_g[0], 16)
    pv[:, 208:224] = _fm(out_norm_b_g[0], 16)
    ident = np.eye(128, dtype=f32)
    tri = np.triu(np.ones((128, 128), f32))
    wsT = np.ascontiguousarray(np.asarray(w_spatial, f32)[0].transpose(2, 0, 1))
    wr = np.ascontiguousarray(np.asarray(w_r, f32)[0].transpose(1, 0, 2))
    wi = np.ascontiguousarray(np.asarray(w_i, f32)[0].transpose(1, 0, 2))
    bsp = np.ascontiguousarray(np.asarray(b_spatial, f32)[0].reshape(1, 2048))
    gvb = np.ascontiguousarray(np.broadcast_to(np.asarray(gmlp_v_norm_g, f32)[0][None, :], (128, 2048)))
    gpm = np.ascontiguousarray(np.broadcast_to(np.asarray(post_mix_g, f32)[0][None, :], (128, D)))
    gpf = np.ascontiguousarray(np.broadcast_to(np.asarray(post_ffn_g, f32)[0][None, :], (128, D)))
    zeros = np.zeros((NT, D), f32)
    in_maps = []
    for c in range(8):
        b, hf = c // 2, c % 2
        m = {
            "x_own": np.ascontiguousarray(x[b, hf * NT:(hf + 1) * NT]),
            "x_prev": np.ascontiguousarray(x[b, 0:NT]) if hf == 1 else zeros,
            "w_in": w_in0, "w_out": w_out0, "w_ffn_in": w_fi0, "w_ffn_out": w_fo0,
            "pv": pv, "flag": np.full((128, 1), float(hf), f32), "ident": ident, "tri": tri,
            "wsT": wsT, "wr": wr, "wi": wi, "bsp": bsp, "gvb": gvb, "gpm": gpm, "gpf": gpf,
        }
        in_maps.append(m)
    if "nc" not in _NC_CACHE:
        _NC_CACHE["nc"] = build_program()
    nc = _NC_CACHE["nc"]
    res = run_bass_kernel_spmd(nc, in_maps, core_ids=list(range(8)))
    out = np.empty((B, SEQ, D), f32)
    for c in range(8):
        b, hf = c // 2, c % 2
        out[b, hf * NT:(hf + 1) * NT] = np.asarray(res.results[c]["out"], f32)
    return out
```

```python
import contextlib
import numpy as np
import concourse.bass as bass
import concourse.mybir as mybir
from concourse.bass_utils import run_bass_kernel_spmd

F32 = mybir.dt.float32
BF16 = mybir.dt.bfloat16
AF = mybir.ActivationFunctionType
ALU = mybir.AluOpType
AX = mybir.AxisListType

D = 4096
NT = 1024
DA = 2048
DB = 2048
DIN = 8192
DFF = 11008
NFB = DFF // 128
KC = D // 128
EPS = 1e-6
C1 = 0.7978845608028654
C3 = C1 * 0.044715
SQC3 = C3 ** 0.5
NV = 224
EPOCH = 3000
STOP_AFTER = None


class Op:
    __slots__ = ("eng", "fn", "deps", "dma", "mark", "sem", "val", "gi")

    def __init__(self, eng, fn, dma):
        self.eng = eng
        self.fn = fn
        self.dma = dma
        self.deps = set()
        self.mark = False
        self.sem = None
        self.val = 0


class Sched:
    def __init__(self):
        self.ops = []
        self.last_w = {}
        self.readers = {}
        self.group_keys = set()
        self.ar = []

    def _alias(self, x):
        if isinstance(x, tuple) and x[0] == "ar":
            if x not in self.ar:
                self.ar.append(x)
            return [y for y in self.ar if y[1] < x[2] and x[1] < y[2]]
        return (x,)

    def op(self, eng, fn, r=(), w=(), dma=None):
        o = Op(eng, fn, dma)
        o.gi = len(self.ops)
        deps = o.deps
        for x0 in r:
            for x in self._alias(x0):
                p = self.last_w.get(x)
                if p is not None:
                    deps.add(p)
        for x0 in w:
            for x in self._alias(x0):
                p = self.last_w.get(x)
                if p is not None:
                    deps.add(p)
                for rd in self.readers.get(x, ()):
                    deps.add(rd)
        for x in r:
            self.readers.setdefault(x, []).append(o)
        for x in w:
            self.last_w[x] = o
            self.readers[x] = []
        deps.discard(o)
        self.ops.append(o)
        return o

    def inherit(self, new, olds):
        rs = self.readers.setdefault(new, [])
        for x in olds:
            p = self.last_w.get(x)
            if p is not None:
                rs.append(p)
            rs.extend(self.readers.get(x, ()))

    def emit(self, nc, stack):
        engs = ["pe", "act", "dve", "pool", "sp"]
        per = {e: [o for o in self.ops if o.eng == e] for e in engs}
        for o in self.ops:
            for p in o.deps:
                if p.dma is None:
                    if p.eng == "pe" and o.eng == "pe":
                        continue
                    p.mark = True
        sems = {}

        def getsem(key):
            if key not in sems:
                sems[key] = stack.enter_context(nc.semaphore("s%d" % len(sems)))
            return sems[key]

        for e in engs:
            cnt = 0
            for o in per[e]:
                if o.dma is None and o.mark:
                    ep, v = divmod(cnt, EPOCH)
                    o.sem = getsem(("eng", e, ep))
                    o.val = v + 1
                    cnt += 1
        dcount = {}
        for o in self.ops:
            if o.dma is not None:
                dcount[o.dma] = dcount.get(o.dma, 0) + 1
                o.sem = getsem(("dma", o.dma))
                o.val = 16 * dcount[o.dma]
        for o in self.ops:
            if o.dma is not None and o.dma in self.group_keys:
                o.val = 16 * dcount[o.dma]

        block = stack.enter_context(nc.Block())

        def run(engobj, ops):
            waited = {}
            for o in ops:
                need = {}
                for p in o.deps:
                    if p.dma is None and p.eng == "pe" and o.eng == "pe":
                        continue
                    k = id(p.sem)
                    if k not in need or need[k][1] < p.val:
                        need[k] = (p.sem, p.val)
                for k, (s, v) in need.items():
                    if waited.get(k, 0) < v:
                        engobj.wait_ge(s, v)
                        waited[k] = v
                ins = o.fn(engobj) if o.fn is not None else None
                if o.dma is not None:
                    ins.then_inc(o.sem, 16)
                elif o.mark:
                    ins.then_inc(o.sem, 1)

        @block.tensor
        def _(e):
            run(e, per["pe"])

        @block.scalar
        def _(e):
            run(e, per["act"])

        @block.vector
        def _(e):
            run(e, per["dve"])

        @block.gpsimd
        def _(e):
            run(e, per["pool"])

        @block.sync
        def _(e):
            run(e, per["sp"])


class T:
    def __init__(self, ap, r):
        self.ap = ap
        self.r = r


def build_program():
    nc = bass.Bass("TRN2", target_bir_lowering=False)
    stack = contextlib.ExitStack()
    S = Sched()
    HTQ = [("HT", q) for q in range(8)]
    H2TQ = [("H2T", q) for q in range(8)]

    def din(name, shape, dt=F32):
        return nc.dram_tensor(name, list(shape), dt, kind="ExternalInput").ap()

    x_own = din("x_own", [NT, D])
    x_prev = din("x_prev", [NT, D])
    w_in = din("w_in", [D, DIN])
    w_out = din("w_out", [D, D])
    w_fi = din("w_ffn_in", [D, 2 * DFF])
    w_fo = din("w_ffn_out", [DFF, D])
    pv_d = din("pv", [128, NV])
    flag_d = din("flag", [128, 1])
    ident_d = din("ident", [128, 128])
    tri_d = din("tri", [128, 128])
    wsT_d = din("wsT", [128, 16, 128])
    wr_d = din("wr", [128, 16, 128])
    wi_d = din("wi", [128, 16, 128])
    bsp_d = din("bsp", [1, 2048])
    gv_d = din("gvb", [128, 2048])
    gpm_d = din("gpm", [128, D])
    gpf_d = din("gpf", [128, D])
    out_d = nc.dram_tensor("out", [NT, D], F32, kind="ExternalOutput").ap()
    oscr = nc.dram_tensor("oscr", [NT, D], F32, kind="ExternalOutput").ap()
    fscr = nc.dram_tensor("fscr", [NT, D], F32, kind="ExternalOutput").ap()
    yscr = nc.dram_tensor("yscr", [KC, 128, NT], BF16, kind="ExternalOutput").ap()

    def sb(name, shape, dt=F32):
        return stack.enter_context(nc.sbuf_tensor("sb_" + name, list(shape), dt))

    AW = 12832
    BIG = sb("BIG", [128, KC, NT], BF16)
    RING = sb("RING", [128, 4, 8192], BF16)
    ARENA = sb("ARENA", [128, AW], F32)
    ident = sb("ident", [128, 128])
    pv = sb("pv", [128, NV])
    dv = sb("dv", [128, 160])
    flag = sb("flag", [128, 1])
    cst = sb("cst", [128, 8])
    wsT = sb("wsT", [128, 16, 128], BF16)
    wrb = sb("wrb", [128, 16, 128], BF16)
    wib = sb("wib", [128, 16, 128], BF16)
    onesb = sb("onesb", [128, 128], BF16)
    WSS = sb("wss", [128, 4, 4, 128], BF16)
    bhi = sb("bhi", [1, 2048], BF16)
    blo = sb("blo", [1, 2048], BF16)
    hc = sb("hc", [128, 16])
    xr3 = sb("xr3", [128, 16, 3])
    st = sb("st", [128, 256])
    PS = [stack.enter_context(nc.psum_tensor("ps%d" % i, [128, 512], F32)) for i in range(8)]

    HT = BIG[:, :, 0:512]
    YT = BIG[:, :, 512:1024]

    def A(off, n, dt=F32):
        if dt == F32:
            assert off + n <= AW
            return T(ARENA[:, off:off + n], ("ar", off, off + n))
        assert n % 2 == 0 and off + n // 2 <= AW
        return T(ARENA[:, off:off + n // 2].bitcast(BF16), ("ar", off, off + n // 2))

    def slot_view(s, kind, nb=None):
        flat = RING[:, s, :]
        if kind == "k32n256":
            return flat.rearrange("p (k n) -> p k n", n=256)
        if kind == "k16n512":
            return flat.rearrange("p (k n) -> p k n", n=512)
        if kind == "fo":
            return flat[:, 0:nb * 512].rearrange("p (k n) -> p k n", n=512)
        raise ValueError(kind)

    ring_ctr = [0]

    def ring_load(src_ap, kind, nb=None):
        s = ring_ctr[0] % 4
        ring_ctr[0] += 1
        view = slot_view(s, kind, nb)
        S.op("pool", lambda e, v=view, a=src_ap: e.dma_start(out=v, in_=a),
             w=[("ring", s)], dma=("ring", s))
        return s, view

    PV_GPRE, PV_GFFN, PV_CW, PV_BC, PV_BR, PV_BI, PV_LAM, PV_GA, PV_GB = 0, 32, 64, 128, 144, 160, 176, 192, 208
    DV_HBR, DV_HBI, DV_CL, DV_HCL, DV_QGA, DV_IQGA, DV_HGB, DV_IHGB, DV_T0, DV_T1 = 0, 16, 32, 48, 64, 80, 96, 112, 128, 144
    C_EPS, C_EPS4, C_ONE, C_ZERO, C_Q = 0, 1, 2, 3, 4

    S.group_keys.add("const")
    S.group_keys.add("constp")

    def cload(dst, src, res):
        S.op("sp", lambda e: e.dma_start(out=dst, in_=src), w=[res], dma="const")

    WST = A(0, 2048)
    TRI = A(2048, 128)
    BSP = A(2176, 2048)
    BT1 = A(4224, 2048)
    cload(ident[:, :], ident_d, "ident")
    cload(TRI.ap, tri_d, TRI.r)
    cload(pv[:, :], pv_d, "pv")
    cload(flag[:, :], flag_d, "flag")
    cload(BSP.ap[0:1, :], bsp_d, BSP.r)
    cload(WST.ap.rearrange("p (h t) -> p h t", t=128), wsT_d, WST.r)
    S.op("pool", lambda e: e.dma_start(out=wrb[:, :, :], in_=wr_d), w=["wrb"], dma="constp")
    S.op("pool", lambda e: e.dma_start(out=wib[:, :, :], in_=wi_d), w=["wib"], dma="constp")

    S.op("dve", lambda e: e.memset(cst[:, C_EPS:C_EPS + 1], EPS), w=["cst"])
    S.op("dve", lambda e: e.memset(cst[:, C_EPS4:C_EPS4 + 1], 4 * EPS), w=["cst"])
    S.op("dve", lambda e: e.memset(cst[:, C_ONE:C_ONE + 1], 1.0), w=["cst"])
    S.op("dve", lambda e: e.memset(cst[:, C_ZERO:C_ZERO + 1], 0.0), w=["cst"])
    S.op("dve", lambda e: e.memset(cst[:, C_Q:C_Q + 1], 0.25), w=["cst"])
    S.op("dve", lambda e: e.memset(onesb[:, :], 1.0), w=["onesb"])
    S.op("dve", lambda e: e.memset(hc[:, :], 0.0), w=["hc"])
    S.op("dve", lambda e: e.memset(xr3[:, :, :], 0.0), w=["xr3"])
    S.op("dve", lambda e: e.tensor_tensor(
        out=wsT[:, :, :], in0=WST.ap.rearrange("p (h t) -> p h t", t=128),
        in1=TRI.ap.unsqueeze(1).to_broadcast([128, 16, 128]), op=ALU.mult),
        r=[WST.r, TRI.r], w=["wsT"])
    b0 = BSP.ap[0:1, :]
    b1 = BT1.ap[0:1, :]
    S.op("dve", lambda e: e.tensor_copy(out=bhi[:, :], in_=b0), r=[BSP.r], w=["bhi"])
    S.op("dve", lambda e: e.tensor_copy(out=b1, in_=bhi[:, :]), r=["bhi"], w=[BT1.r])
    S.op("dve", lambda e: e.tensor_tensor(out=b1, in0=b0, in1=b1, op=ALU.subtract), r=[BSP.r, BT1.r], w=[BT1.r])
    S.op("dve", lambda e: e.tensor_copy(out=blo[:, :], in_=b1), r=[BT1.r], w=["blo"])

    def dvs(c):
        return dv[:, c:c + 16]

    def pvs(c):
        return pv[:, c:c + 16]

    S.op("dve", lambda e: e.tensor_scalar(out=dvs(DV_HBR), in0=pvs(PV_BR), scalar1=0.5, scalar2=None, op0=ALU.mult),
         r=["pv"], w=["dv_hbr"])
    S.op("dve", lambda e: e.tensor_scalar(out=dvs(DV_HBI), in0=pvs(PV_BI), scalar1=0.5, scalar2=None, op0=ALU.mult),
         r=["pv"], w=["dv_hbi"])
    S.op("dve", lambda e: e.tensor_scalar(out=dvs(DV_QGA), in0=pvs(PV_GA), scalar1=0.5, scalar2=None, op0=ALU.mult),
         r=["pv"], w=["dv_qga"])
    S.op("dve", lambda e: e.reciprocal(out=dvs(DV_IQGA), in_=dvs(DV_QGA)), r=["dv_qga"], w=["dv_iqga"])
    S.op("dve", lambda e: e.tensor_scalar(out=dvs(DV_HGB), in0=pvs(PV_GB), scalar1=0.5, scalar2=None, op0=ALU.mult),
         r=["pv"], w=["dv_hgb"])
    S.op("dve", lambda e: e.reciprocal(out=dvs(DV_IHGB), in_=dvs(DV_HGB)), r=["dv_hgb"], w=["dv_ihgb"])
    S.op("dve", lambda e: e.tensor_scalar(out=dvs(DV_T0), in0=pvs(PV_LAM), scalar1=-1.0, scalar2=None, op0=ALU.mult),
         r=["pv"], w=["dv_t0"])
    S.op("dve", lambda e: e.tensor_tensor(out=dvs(DV_T0), in0=dvs(DV_T0), in1=pvs(PV_LAM), op=ALU.max),
         r=["pv", "dv_t0"], w=["dv_t0"])
    S.op("act", lambda e: e.activation(out=dvs(DV_T0), in_=dvs(DV_T0), func=AF.Exp, scale=-1.0),
         r=["dv_t0"], w=["dv_t0"])
    S.op("act", lambda e: e.activation(out=dvs(DV_T0), in_=dvs(DV_T0), func=AF.Ln, bias=cst[:, C_ONE:C_ONE + 1], scale=1.0),
         r=["dv_t0", "cst"], w=["dv_t0"])
    S.op("dve", lambda e: e.tensor_scalar(out=dvs(DV_T1), in0=pvs(PV_LAM), scalar1=-1.0, scalar2=0.0, op0=ALU.mult, op1=ALU.max),
         r=["pv"], w=["dv_t1"])
    S.op("dve", lambda e: e.tensor_tensor(out=dvs(DV_T0), in0=dvs(DV_T0), in1=dvs(DV_T1), op=ALU.add),
         r=["dv_t0", "dv_t1"], w=["dv_t0"])
    S.op("dve", lambda e: e.tensor_scalar(out=dvs(DV_CL), in0=dvs(DV_T0), scalar1=-8.0, scalar2=None, op0=ALU.mult),
         r=["dv_t0"], w=["dv_cl"])
    S.op("dve", lambda e: e.tensor_scalar(out=dvs(DV_HCL), in0=dvs(DV_T0), scalar1=-4.0, scalar2=None, op0=ALU.mult),
         r=["dv_t0"], w=["dv_hcl"])

    def rstd_small(dst, src, scale, eps_col, r, w):
        S.op("act", lambda e: e.activation(out=dst, in_=src, func=AF.Sqrt, bias=cst[:, eps_col:eps_col + 1], scale=scale),
             r=list(r) + ["cst"], w=list(w))
        S.op("dve", lambda e: e.reciprocal(out=dst, in_=dst), r=list(w), w=list(w))

    tp_ctr = [0]

    def transpose_to_fm(xn, gcol, dest_fn, dest_res, src_res=None):
        for q in range(8):
            b = tp_ctr[0] % 8
            tp_ctr[0] += 1
            bank = PS[b]

            def mm(e, q=q, bank=bank):
                ins = None
                for j in range(4):
                    kc = q * 4 + j
                    ins = e.transpose(bank[:, j * 128:(j + 1) * 128], xn.ap[:, kc * 128:(kc + 1) * 128], ident[:, :])
                return ins
            S.op("pe", mm, r=[xn.r if src_res is None else src_res(q), "ident"], w=[("ps", b)])
            S.op("dve", lambda e, q=q, bank=bank: e.tensor_tensor(
                out=dest_fn(q), in0=bank[:, :].rearrange("p (j t) -> p j t", t=128),
                in1=pv[:, gcol + q * 4:gcol + q * 4 + 4].unsqueeze(2).to_broadcast([128, 4, 128]), op=ALU.mult),
                r=[("ps", b), "pv"], w=[(dest_res, q)])

    XT = [A(0, 4096), A(4096, 4096)]
    JUNK = A(8192, 4096, BF16)

    def phase1(x_ap):
        for tg in range(8):
            xt = XT[tg % 2]
            r0 = tg * 128
            c = 208 + 2 * (tg % 2)
            S.op("sp", lambda e, xt=xt, r0=r0: e.dma_start(out=xt.ap, in_=x_ap[r0:r0 + 128, :]),
                 w=[xt.r], dma=("xt", tg % 2))
            S.op("act", lambda e, xt=xt, c=c: e.activation(out=JUNK.ap, in_=xt.ap, func=AF.Square, accum_out=st[:, c:c + 1]),
                 r=[xt.r], w=[JUNK.r, ("p1s", tg % 2)])
            rstd_small(st[:, c + 1:c + 2], st[:, c:c + 1], 1.0 / D, C_EPS, [("p1s", tg % 2)], [("p1r", tg % 2)])
            lo_r = ("ar", xt.r[1], xt.r[1] + 2048)
            hi_r = ("ar", xt.r[1] + 2048, xt.r[2])
            S.op("act", lambda e, xt=xt, c=c: e.activation(out=xt.ap[:, 0:2048], in_=xt.ap[:, 0:2048], func=AF.Identity,
                                                           scale=st[:, c + 1:c + 2]),
                 r=[lo_r, ("p1r", tg % 2)], w=[lo_r])
            S.op("dve", lambda e, xt=xt, c=c: e.tensor_scalar(out=xt.ap[:, 2048:4096], in0=xt.ap[:, 2048:4096],
                                                              scalar1=st[:, c + 1:c + 2], scalar2=None, op0=ALU.mult),
                 r=[hi_r, ("p1r", tg % 2)], w=[hi_r])
            transpose_to_fm(xt, PV_GPRE, lambda q, tg=tg: BIG[:, q * 4:q * 4 + 4, tg * 128:(tg + 1) * 128], "HT",
                            src_res=lambda q, lo_r=lo_r, hi_r=hi_r: lo_r if q < 4 else hi_r)

    GVB = A(0, 2048)
    o_ = 2048
    XRB = A(o_, 516); o_ += 516
    XC = [A(o_, 512), A(o_ + 512, 512)]; o_ += 1024
    GG = [A(o_, 512), A(o_ + 512, 512)]; o_ += 1024
    XCB = [A(o_, 512, BF16), A(o_ + 256, 512, BF16)]; o_ += 512
    YSQ = [A(o_, 512, BF16), A(o_ + 256, 512, BF16)]; o_ += 512
    YST = [A(o_, 512, BF16), A(o_ + 256, 512, BF16)]; o_ += 512
    TMP = A(o_, 512); o_ += 512
    TR = A(o_, 512); o_ += 512
    TI = A(o_, 512); o_ += 512
    AA = A(o_, 512); o_ += 512
    A2 = A(o_, 512); o_ += 512
    HH = A(o_, 512); o_ += 512
    VN = A(o_, 2048, BF16); o_ += 1024
    GU = [A(o_ + i * 512, 512) for i in range(4)]; o_ += 2048
    GV = A(o_, 512); o_ += 512
    assert o_ <= AW, o_
    SSQ = PS[7]

    def y_store(yst, kc, half):
        S.op("sp", lambda e: e.dma_start(out=yscr[kc, :, half * 512:(half + 1) * 512], in_=yst.ap),
             r=[yst.r], w=[("yscr", kc)], dma=("yst", yst.r[1]))

    def gelu2(dst, src_ps, src_res):
        S.op("act", lambda e: e.activation(out=TMP.ap, in_=src_ps, func=AF.Square, scale=SQC3), r=[src_res], w=[TMP.r])
        S.op("dve", lambda e: e.scalar_tensor_tensor(out=TMP.ap, in0=TMP.ap, scalar=C1, in1=src_ps, op0=ALU.add, op1=ALU.mult),
             r=[TMP.r, src_res], w=[TMP.r])
        S.op("act", lambda e: e.activation(out=TMP.ap, in_=TMP.ap, func=AF.Tanh), r=[TMP.r], w=[TMP.r])
        S.op("dve", lambda e: e.scalar_tensor_tensor(out=dst.ap, in0=TMP.ap, scalar=1.0, in1=src_ps, op0=ALU.add, op1=ALU.mult),
             r=[TMP.r, src_res], w=[dst.r])

    def proj_fm(bank_i, slot, view, jj, half):
        bank = PS[bank_i]

        def mm(e):
            ins = None
            for kc in range(KC):
                ins = e.matmul(bank[:, :], lhsT=view[:, kc, jj * 128:(jj + 1) * 128],
                               rhs=BIG[:, kc, half * 512:(half + 1) * 512],
                               start=(kc == 0), stop=(kc == KC - 1))
            return ins
        S.op("pe", mm, r=[("ring", slot)] + HTQ, w=[("ps", bank_i)])

    def ssq_mm(ysq, colbase, half):
        def mm(e):
            ins = None
            for t in range(4):
                c = (half * 4 + t) * 32 + colbase
                ins = e.matmul(SSQ[:, c:c + 1], lhsT=ysq.ap[:, t * 128:(t + 1) * 128], rhs=onesb[:, 0:1],
                               start=True, stop=True)
            return ins
        S.op("pe", mm, r=[ysq.r, "onesb"], w=[("ps", 7)])

    def gmlp():
        pend = []

        def flush(keep_res=None):
            nonlocal pend
            rest = []
            for (yq, col, hf_) in pend:
                if keep_res is not None and yq.r != keep_res:
                    rest.append((yq, col, hf_))
                else:
                    ssq_mm(yq, col, hf_)
            pend = rest

        SSV = st[:, 216:232]
        k_ = 0
        for hg in range(4):
            sA, vA = ring_load(w_in[0:2048, DA + hg * 512:DA + (hg + 1) * 512].rearrange("(k p) n -> p k n", p=128), "k16n512")
            sB, vB = ring_load(w_in[2048:4096, DA + hg * 512:DA + (hg + 1) * 512].rearrange("(k p) n -> p k n", p=128), "k16n512")
            uslots = []
            for j in range(2):
                c0 = hg * 512 + j * 256
                uslots.append(ring_load(w_in[:, c0:c0 + 256].rearrange("(k p) n -> p k n", p=128), "k32n256"))
            for half in range(2):
                for t in range(4):
                    tg = half * 4 + t

                    def mm(e, t=t, tg=tg, vA=vA, vB=vB):
                        ins = None
                        for k in range(KC):
                            vw = vA if k < 16 else vB
                            ins = e.matmul(PS[t][:, :], lhsT=BIG[:, k, tg * 128:(tg + 1) * 128], rhs=vw[:, k % 16, :],
                                           start=(k == 0), stop=(k == KC - 1))
                        return ins
                    S.op("pe", mm, r=[("ring", sA), ("ring", sB)] + HTQ, w=[("ps", t)])
                    if t == 0:
                        flush()
                for t in range(4):
                    gelu2(GV, PS[t][:, :], ("ps", t))
                    S.op("act", lambda e: e.activation(out=TMP.ap, in_=GV.ap, func=AF.Square), r=[GV.r], w=[TMP.r])
                    S.op("dve", lambda e, t=t: e.tensor_reduce(out=SSV[:, t * 4:(t + 1) * 4],
                                                               in_=TMP.ap.rearrange("p (h d) -> p h d", d=128),
                                                               axis=AX.X, op=ALU.add), r=[TMP.r], w=["ssv"])
                    S.op("dve", lambda e, t=t, hg=hg: e.tensor_tensor(
                        out=VN.ap[:, t * 512:(t + 1) * 512], in0=GV.ap, in1=GVB.ap[:, hg * 512:(hg + 1) * 512], op=ALU.mult),
                        r=[GV.r, GVB.r], w=[VN.r])
                rstd_small(SSV, SSV, 1.0 / 128, C_EPS4, ["ssv"], ["ssv"])
                for t in range(4):
                    S.op("dve", lambda e, t=t, hg=hg: e.tensor_tensor(
                        out=WSS[:, t, :, :], in0=wsT[:, hg * 4:(hg + 1) * 4, :],
                        in1=SSV[:, t * 4:(t + 1) * 4].unsqueeze(2).to_broadcast([128, 4, 128]), op=ALU.mult),
                        r=["wsT", "ssv"], w=["WSS"])
                for hl in range(4):
                    sl, vw = uslots[hl // 2]
                    ub = 4 + (hl % 2)
                    proj_fm(ub, sl, vw, hl % 2, half)
                    gelu2(GU[hl], PS[ub][:, :], ("ps", ub))
                for hl in range(4):
                    h = hg * 4 + hl

                    def mm(e, hl=hl, h=h):
                        ins = None
                        for t in range(4):
                            o = PS[hl][:, t * 128:(t + 1) * 128]
                            e.matmul(o, lhsT=VN.ap[:, t * 512 + hl * 128:t * 512 + (hl + 1) * 128], rhs=WSS[:, t, hl, :],
                                     start=True, stop=False)
                            e.matmul(o, lhsT=onesb[0:1, :], rhs=bhi[0:1, h * 128:(h + 1) * 128], start=False, stop=False)
                            ins = e.matmul(o, lhsT=onesb[0:1, :], rhs=blo[0:1, h * 128:(h + 1) * 128], start=False, stop=True)
                        return ins
                    S.op("pe", mm, r=[VN.r, "WSS", "onesb", "bhi", "blo"], w=[("ps", hl)])
                    yst = YST[k_ % 2]
                    yq = YSQ[k_ % 2]
                    k_ += 1
                    S.op("dve", lambda e, h=h, hl=hl, yst=yst: e.scalar_tensor_tensor(
                        out=yst.ap, in0=PS[hl][:, :], scalar=dv[:, DV_QGA + h:DV_QGA + h + 1], in1=GU[hl].ap,
                        op0=ALU.mult, op1=ALU.mult), r=[("ps", hl), GU[hl].r, "dv_qga"], w=[yst.r])
                    y_store(yst, h, half)
                    flush(keep_res=yq.r)
                    S.op("act", lambda e, h=h, yq=yq, yst=yst: e.activation(out=yq.ap, in_=yst.ap, func=AF.Square,
                                                                          scale=dv[:, DV_IQGA + h:DV_IQGA + h + 1]),
                         r=[yst.r, "dv_iqga"], w=[yq.r])
                    pend.append((yq, h, half))
        flush()

    def rglru(main):
        slots = {}
        NS = 32
        GB = [2, 3, 6]

        def pe0(s):
            n, half = s // 2, s % 2
            if s % 4 == 0:
                c0 = DA * 2 + DB + n * 128
                slots["x"] = ring_load(w_in[:, c0:c0 + 256].rearrange("(k p) n -> p k n", p=128), "k32n256")
                if main:
                    c0 = DA * 2 + n * 128
                    slots["g"] = ring_load(w_in[:, c0:c0 + 256].rearrange("(k p) n -> p k n", p=128), "k32n256")
            if main:
                proj_fm(GB[s % 3], slots["g"][0], slots["g"][1], n % 2, half)
            proj_fm(s % 2, slots["x"][0], slots["x"][1], n % 2, half)

        def a1(s):
            n, par = s // 2, s % 2
            xps = PS[par][:, :]
            xc = XC[par]
            S.op("act", lambda e: e.activation(out=XRB.ap[:, 3:515], in_=xps, func=AF.Copy), r=[("ps", par)], w=[XRB.r])
            S.op("act", lambda e: e.activation(out=xc.ap, in_=xps, func=AF.Identity,
                                               bias=pv[:, PV_BC + n:PV_BC + n + 1],
                                               scale=pv[:, PV_CW + n * 4 + 3:PV_CW + n * 4 + 4]),
                 r=[("ps", par), "pv"], w=[xc.r])
            if main:
                gb = GB[s % 3]
                S.op("act", lambda e: e.activation(out=GG[par].ap, in_=PS[gb][:, :], func=AF.Square, scale=SQC3),
                     r=[("ps", gb)], w=[GG[par].r])

        def d1(s):
            n, par = s // 2, s % 2
            xc = XC[par]
            S.op("dve", lambda e: e.tensor_copy(out=XRB.ap[:, 0:3], in_=xr3[:, n, :]), r=["xr3"], w=[XRB.r])
            for k in (2, 1, 0):
                S.op("dve", lambda e, k=k: e.scalar_tensor_tensor(
                    out=xc.ap, in0=XRB.ap[:, k:k + 512], scalar=pv[:, PV_CW + n * 4 + k:PV_CW + n * 4 + k + 1],
                    in1=xc.ap, op0=ALU.mult, op1=ALU.add), r=[XRB.r, xc.r, "pv"], w=[xc.r])
            S.op("dve", lambda e: e.tensor_copy(out=xr3[:, n, :], in_=XRB.ap[:, 512:515]), r=[XRB.r], w=["xr3"])
            S.op("dve", lambda e: e.tensor_copy(out=XCB[par].ap, in_=xc.ap), r=[xc.r], w=[XCB[par].r])
            if main:
                gb = GB[s % 3]
                S.op("dve", lambda e: e.scalar_tensor_tensor(out=GG[par].ap, in0=GG[par].ap, scalar=C1, in1=PS[gb][:, :],
                                                             op0=ALU.add, op1=ALU.mult),
                     r=[GG[par].r, ("ps", gb)], w=[GG[par].r])

        def pe1(s):
            n, par = s // 2, s % 2

            def mm(e):
                e.matmul(PS[4][:, :], lhsT=wrb[:, n, :], rhs=XCB[par].ap, start=True, stop=True)
                return e.matmul(PS[5][:, :], lhsT=wib[:, n, :], rhs=XCB[par].ap, start=True, stop=True)
            S.op("pe", mm, r=["wrb", "wib", XCB[par].r], w=[("ps", 4), ("ps", 5)])

        def a2(s):
            n, par = s // 2, s % 2
            if main:
                S.op("act", lambda e: e.activation(out=GG[par].ap, in_=GG[par].ap, func=AF.Tanh), r=[GG[par].r], w=[GG[par].r])
            S.op("act", lambda e: e.activation(out=TR.ap, in_=PS[4][:, :], func=AF.Tanh,
                                               bias=dv[:, DV_HBR + n:DV_HBR + n + 1], scale=0.5),
                 r=[("ps", 4), "dv_hbr"], w=[TR.r])
            S.op("act", lambda e: e.activation(out=TI.ap, in_=PS[5][:, :], func=AF.Tanh,
                                               bias=dv[:, DV_HBI + n:DV_HBI + n + 1], scale=0.5),
                 r=[("ps", 5), "dv_hbi"], w=[TI.r])
            S.op("act", lambda e: e.activation(out=AA.ap, in_=TR.ap, func=AF.Exp, bias=dv[:, DV_HCL + n:DV_HCL + n + 1],
                                               scale=dv[:, DV_HCL + n:DV_HCL + n + 1]), r=[TR.r, "dv_hcl"], w=[AA.r])
            S.op("act", lambda e: e.activation(out=A2.ap, in_=TR.ap, func=AF.Exp, bias=dv[:, DV_CL + n:DV_CL + n + 1],
                                               scale=dv[:, DV_CL + n:DV_CL + n + 1]), r=[TR.r, "dv_cl"], w=[A2.r])
            S.op("act", lambda e: e.activation(out=A2.ap, in_=A2.ap, func=AF.Sqrt, bias=cst[:, C_Q:C_Q + 1], scale=-0.25),
                 r=[A2.r, "cst"], w=[A2.r])

        def d2(s):
            n, half, par = s // 2, s % 2, s % 2
            xc = XC[par]
            if main:
                gb = GB[s % 3]
                S.op("dve", lambda e: e.scalar_tensor_tensor(out=GG[par].ap, in0=GG[par].ap, scalar=1.0, in1=PS[gb][:, :],
                                                             op0=ALU.add, op1=ALU.mult),
                     r=[GG[par].r, ("ps", gb)], w=[GG[par].r])
            S.op("dve", lambda e: e.scalar_tensor_tensor(out=TI.ap, in0=TI.ap, scalar=1.0, in1=xc.ap,
                                                         op0=ALU.add, op1=ALU.mult), r=[TI.r, xc.r], w=[TI.r])
            S.op("dve", lambda e: e.scalar_tensor_tensor(out=TI.ap, in0=A2.ap, scalar=0.5e-6, in1=TI.ap,
                                                         op0=ALU.max, op1=ALU.mult), r=[TI.r, A2.r], w=[TI.r])
            S.op("dve", lambda e: e.tensor_tensor_scan(out=HH.ap, data0=AA.ap, data1=TI.ap, initial=hc[:, n:n + 1],
                                                       op0=ALU.mult, op1=ALU.add), r=[AA.r, TI.r, "hc"], w=[HH.r])
            S.op("dve", lambda e: e.tensor_copy(out=hc[:, n:n + 1], in_=HH.ap[:, 511:512]), r=[HH.r], w=["hc"])
            if main:
                yst = YST[par]
                S.op("dve", lambda e: e.scalar_tensor_tensor(
                    out=yst.ap, in0=HH.ap, scalar=dv[:, DV_HGB + n:DV_HGB + n + 1], in1=GG[par].ap,
                    op0=ALU.mult, op1=ALU.mult), r=[HH.r, GG[par].r, "dv_hgb"], w=[yst.r])
                y_store(yst, 16 + n, half)

        def a3(s):
            n, par = s // 2, s % 2
            S.op("act", lambda e: e.activation(out=YSQ[par].ap, in_=YST[par].ap, func=AF.Square,
                                               scale=dv[:, DV_IHGB + n:DV_IHGB + n + 1]),
                 r=[YST[par].r, "dv_ihgb"], w=[YSQ[par].r])

        for k in range(NS + 3):
            if k < NS:
                pe0(k)
                a1(k)
                d1(k)
            if 1 <= k <= NS:
                pe1(k - 1)
                a2(k - 1)
                d2(k - 1)
            if main and 2 <= k <= NS + 1:
                a3(k - 2)
            if main and 3 <= k <= NS + 2:
                s = k - 3
                ssq_mm(YSQ[s % 2], 16 + s // 2, s % 2)

    OB = [A(2048 + i * 512, 512) for i in range(3)]
    TMPO = [A(3584 + i * 512, 512) for i in range(2)]
    SSO = st[:, 64:128]
    RA = st[:, 16:24]
    RB = st[:, 40:48]

    def phase3a():
        S.group_keys.add("yld")
        for kc in range(KC):
            S.inherit(("YT", kc), HTQ)
            S.op("sp", lambda e, kc=kc: e.dma_start(out=BIG[:, kc, :], in_=yscr[kc, :, :]),
                 r=[("yscr", kc)], w=[("YT", kc)], dma="yld")
        S.op("dve", lambda e: e.tensor_reduce(out=RA, in_=SSQ[:, 0:256].rearrange("p (t c) -> p t c", c=32)[:, :, 0:16],
                                              axis=AX.X, op=ALU.add), r=[("ps", 7)], w=["RA"])
        S.op("dve", lambda e: e.tensor_reduce(out=RB, in_=SSQ[:, 0:256].rearrange("p (t c) -> p t c", c=32)[:, :, 16:32],
                                              axis=AX.X, op=ALU.add), r=[("ps", 7)], w=["RB"])
        rstd_small(RA, RA, 1.0 / DA, C_EPS, ["RA"], ["RA"])
        rstd_small(RB, RB, 1.0 / DB, C_EPS, ["RB"], ["RB"])
        i_ = 0
        for c in range(8):
            sa, va = ring_load(w_out[0:2048, c * 512:(c + 1) * 512].rearrange("(k p) n -> p k n", p=128), "k16n512")
            sb_, vb = ring_load(w_out[2048:4096, c * 512:(c + 1) * 512].rearrange("(k p) n -> p k n", p=128), "k16n512")
            for tg in range(8):
                ba = tg % 2
                bb = 2 + tg % 2

                def mm(e, va=va, vb=vb, tg=tg, ba=ba, bb=bb):
                    ins = None
                    for k in range(16):
                        e.matmul(PS[ba][:, :], lhsT=BIG[:, k, tg * 128:(tg + 1) * 128], rhs=va[:, k, :],
                                 start=(k == 0), stop=(k == 15))
                    for k in range(16):
                        ins = e.matmul(PS[bb][:, :], lhsT=BIG[:, 16 + k, tg * 128:(tg + 1) * 128], rhs=vb[:, k, :],
                                       start=(k == 0), stop=(k == 15))
                    return ins
                S.op("pe", mm, r=[("ring", sa), ("ring", sb_)] + [("YT", k) for k in range(32)],
                     w=[("ps", ba), ("ps", bb)])
                tm = TMPO[i_ % 2]
                ob = OB[i_ % 3]
                S.op("act", lambda e, tm=tm, ba=ba, tg=tg: e.activation(out=tm.ap, in_=PS[ba][:, :], func=AF.Identity,
                                                                        scale=RA[:, tg:tg + 1]),
                     r=[("ps", ba), "RA"], w=[tm.r])
                S.op("dve", lambda e, tm=tm, ob=ob, bb=bb, tg=tg: e.scalar_tensor_tensor(
                    out=ob.ap, in0=PS[bb][:, :], scalar=RB[:, tg:tg + 1], in1=tm.ap, op0=ALU.mult, op1=ALU.add),
                    r=[("ps", bb), "RB", tm.r], w=[ob.r])
                S.op("act", lambda e, tm=tm, ob=ob, tg=tg, c=c: e.activation(
                    out=tm.ap, in_=ob.ap, func=AF.Square, accum_out=SSO[:, tg * 8 + c:tg * 8 + c + 1]),
                    r=[ob.r], w=[tm.r, ("SSO", tg)])
                S.op("sp", lambda e, ob=ob, tg=tg, c=c: e.dma_start(
                    out=oscr[tg * 128:(tg + 1) * 128, c * 512:(c + 1) * 512], in_=ob.ap),
                    r=[ob.r], w=[("oscr", tg)], dma=("obw", i_ % 3))
                i_ += 1

    OT = [A(0, 4096), T(RING[:, 0, :].bitcast(F32), ("ring", 0)), T(RING[:, 2, :].bitcast(F32), ("ring", 2))]
    XT3 = [A(4096, 4096), T(RING[:, 1, :].bitcast(F32), ("ring", 1)), T(RING[:, 3, :].bitcast(F32), ("ring", 3))]
    GP = A(8192, 4096)
    RO = st[:, 24:32]
    RF = st[:, 32:40]
    SSF = st[:, 128:192]

    def phase3b():
        S.op("sp", lambda e: e.dma_start(out=GP.ap, in_=gpm_d), w=[GP.r], dma="gp")

        def loads(tg):
            p = tg % 3
            S.op("sp", lambda e: e.dma_start(out=OT[p].ap, in_=oscr[tg * 128:(tg + 1) * 128, :]),
                 r=[("oscr", tg)], w=[OT[p].r], dma=("otl", p))
            S.op("sp", lambda e: e.dma_start(out=XT3[p].ap, in_=x_own[tg * 128:(tg + 1) * 128, :]),
                 w=[XT3[p].r], dma=("xt3l", p))

        def d1(tg):
            p = tg % 3
            ot, xt3 = OT[p], XT3[p]
            S.op("dve", lambda e: e.tensor_reduce(out=RO[:, tg:tg + 1], in_=SSO[:, tg * 8:(tg + 1) * 8],
                                                  axis=AX.X, op=ALU.add), r=[("SSO", tg)], w=[("RO", tg)])
            rstd_small(RO[:, tg:tg + 1], RO[:, tg:tg + 1], 1.0 / D, C_EPS, [("RO", tg)], [("RO", tg)])
            S.op("dve", lambda e: e.scalar_tensor_tensor(out=ot.ap, in0=ot.ap, scalar=RO[:, tg:tg + 1], in1=GP.ap,
                                                         op0=ALU.mult, op1=ALU.mult),
                 r=[ot.r, ("RO", tg), GP.r], w=[ot.r])
            S.op("dve", lambda e: e.tensor_tensor(out=ot.ap, in0=ot.ap, in1=xt3.ap, op=ALU.add),
                 r=[ot.r, xt3.r], w=[ot.r])
            S.op("sp", lambda e: e.dma_start(out=out_d[tg * 128:(tg + 1) * 128, :], in_=ot.ap),
                 r=[ot.r], w=[("outd", tg)], dma=("x1w", p))

        def a_(tg):
            p = tg % 3
            ot, xt3 = OT[p], XT3[p]
            c2 = 200 + 2 * p
            S.op("act", lambda e: e.activation(out=xt3.ap.bitcast(BF16)[:, 0:4096], in_=ot.ap,
                                               func=AF.Square, accum_out=st[:, c2:c2 + 1]),
                 r=[ot.r], w=[xt3.r, ("st2", p)])
            rstd_small(st[:, c2 + 1:c2 + 2], st[:, c2:c2 + 1], 1.0 / D, C_EPS, [("st2", p)], [("st3", p)])
            S.op("act", lambda e: e.activation(out=xt3.ap, in_=ot.ap, func=AF.Identity, scale=st[:, c2 + 1:c2 + 2]),
                 r=[ot.r, ("st3", p)], w=[xt3.r])

        loads(0)
        loads(1)
        loads(2)
        d1(0)
        a_(0)
        for tg in range(8):
            if tg + 1 < 8:
                d1(tg + 1)
            transpose_to_fm(XT3[tg % 3], PV_GFFN, lambda q, tg=tg: BIG[:, q * 4:q * 4 + 4, tg * 128:(tg + 1) * 128], "H2T")
            if tg + 3 < 8:
                loads(tg + 3)
            if tg + 1 < 8:
                a_(tg + 1)

    ACTG_T = A(0, 7168)
    ACTG = ACTG_T.ap.bitcast(BF16).rearrange("p (k t) -> p k t", t=1024)
    SG = [A(7168, 512), A(7680, 512)]
    NFIN = 5
    FIN = [A(8192 + i * 512, 512) for i in range(NFIN)]
    FS = [A(10752 + i * 512, 512) for i in range(3)]
    SQJ = A(12288, 512)

    def ffn():
        groups = []
        j = 0
        for nb in [12] * 6 + [14]:
            groups.append((j, nb))
            j += nb
        assert j == NFB
        ti = [0]
        for gi, (j0, nb) in enumerate(groups):
            last = gi == len(groups) - 1
            for pj in range(nb // 2):
                ja = j0 + pj * 2
                sg_, vg = ring_load(w_fi[:, ja * 128:ja * 128 + 256].rearrange("(k p) n -> p k n", p=128), "k32n256")
                su_, vu = ring_load(w_fi[:, DFF + ja * 128:DFF + ja * 128 + 256].rearrange("(k p) n -> p k n", p=128), "k32n256")
                for jj in range(2):
                    jl = pj * 2 + jj
                    for hf in range(2):
                        for (bank_i, sl, vw) in ((hf, sg_, vg), (2 + hf, su_, vu)):
                            def mm(e, bank_i=bank_i, vw=vw, jj=jj, hf=hf):
                                ins = None
                                for kc in range(KC):
                                    ins = e.matmul(PS[bank_i][:, :], lhsT=vw[:, kc, jj * 128:(jj + 1) * 128],
                                                   rhs=BIG[:, kc, hf * 512:(hf + 1) * 512],
                                                   start=(kc == 0), stop=(kc == KC - 1))
                                return ins
                            S.op("pe", mm, r=[("ring", sl)] + H2TQ, w=[("ps", bank_i)])
                        S.op("act", lambda e, hf=hf: e.activation(out=SG[hf].ap, in_=PS[hf][:, :], func=AF.Silu),
                             r=[("ps", hf)], w=[SG[hf].r])
                        S.op("dve", lambda e, hf=hf, jl=jl: e.tensor_tensor(
                            out=ACTG[:, jl, hf * 512:(hf + 1) * 512], in0=SG[hf].ap, in1=PS[2 + hf][:, :], op=ALU.mult),
                            r=[SG[hf].r, ("ps", 2 + hf)], w=[ACTG_T.r])
            tiles = [(c, tg) for c in range(8) for tg in range(8)]
            if gi > 0:
                for pre in range(NFIN - 1):
                    c, tg = tiles[pre]
                    fb = (ti[0] + pre) % NFIN
                    S.op("sp", lambda e, fb=fb, c=c, tg=tg: e.dma_start(
                        out=FIN[fb].ap, in_=fscr[tg * 128:(tg + 1) * 128, c * 512:(c + 1) * 512]),
                        r=[("fscr", tg, c)], w=[FIN[fb].r], dma=("fin", fb))
            vo = None
            so_ = None
            for idx, (c, tg) in enumerate(tiles):
                if tg == 0:
                    so_, vo = ring_load(w_fo[j0 * 128:(j0 + nb) * 128, c * 512:(c + 1) * 512]
                                        .rearrange("(k p) n -> p k n", p=128), "fo", nb)
                fbk = 4 + idx % 4

                def mm(e, vo=vo, tg=tg, fbk=fbk, nb=nb):
                    ins = None
                    for k in range(nb):
                        ins = e.matmul(PS[fbk][:, :], lhsT=ACTG[:, k, tg * 128:(tg + 1) * 128], rhs=vo[:, k, :],
                                       start=(k == 0), stop=(k == nb - 1))
                    return ins
                S.op("pe", mm, r=[("ring", so_), ACTG_T.r], w=[("ps", fbk)])
                fb = ti[0] % 3
                fbi = ti[0] % NFIN
                if gi > 0 and idx + NFIN - 1 < len(tiles):
                    c2, tg2 = tiles[idx + NFIN - 1]
                    fb2 = (ti[0] + NFIN - 1) % NFIN
                    S.op("sp", lambda e, fb2=fb2, c2=c2, tg2=tg2: e.dma_start(
                        out=FIN[fb2].ap, in_=fscr[tg2 * 128:(tg2 + 1) * 128, c2 * 512:(c2 + 1) * 512]),
                        r=[("fscr", tg2, c2)], w=[FIN[fb2].r], dma=("fin", fb2))
                if gi == 0:
                    S.op("act", lambda e, fb=fb, fbk=fbk: e.activation(out=FS[fb].ap, in_=PS[fbk][:, :], func=AF.Copy),
                         r=[("ps", fbk)], w=[FS[fb].r])
                else:
                    S.op("dve", lambda e, fb=fb, fbk=fbk, fbi=fbi: e.tensor_tensor(out=FS[fb].ap, in0=PS[fbk][:, :],
                                                                                   in1=FIN[fbi].ap, op=ALU.add),
                         r=[("ps", fbk), FIN[fbi].r], w=[FS[fb].r])
                if last:
                    S.op("act", lambda e, fb=fb, tg=tg, c=c: e.activation(
                        out=SQJ.ap, in_=FS[fb].ap, func=AF.Square, accum_out=SSF[:, tg * 8 + c:tg * 8 + c + 1]),
                        r=[FS[fb].r], w=[SQJ.r, ("SSF", tg)])
                S.op("sp", lambda e, fb=fb, c=c, tg=tg: e.dma_start(
                    out=fscr[tg * 128:(tg + 1) * 128, c * 512:(c + 1) * 512], in_=FS[fb].ap),
                    r=[FS[fb].r], w=[("fscr", tg, c)], dma=("fsw", fb))
                ti[0] += 1

    def final():
        OTF = OT
        XTF = XT3
        S.op("sp", lambda e: e.dma_start(out=GP.ap, in_=gpf_d), w=[GP.r], dma="gp")

        def loads(tg):
            p = tg % 3
            S.op("sp", lambda e: e.dma_start(out=OTF[p].ap, in_=fscr[tg * 128:(tg + 1) * 128, :]),
                 r=[("fscr", tg, c) for c in range(8)], w=[OTF[p].r], dma=("otl", p))
            S.op("sp", lambda e: e.dma_start(out=XTF[p].ap, in_=out_d[tg * 128:(tg + 1) * 128, :]),
                 r=[("outd", tg)], w=[XTF[p].r], dma=("xt3l", p))

        loads(0)
        loads(1)
        for tg in range(8):
            p = tg % 3
            ot, xt3 = OTF[p], XTF[p]
            if tg + 2 < 8:
                loads(tg + 2)
            S.op("dve", lambda e, tg=tg: e.tensor_reduce(out=RF[:, tg:tg + 1], in_=SSF[:, tg * 8:(tg + 1) * 8],
                                                         axis=AX.X, op=ALU.add), r=[("SSF", tg)], w=[("RF", tg)])
            rstd_small(RF[:, tg:tg + 1], RF[:, tg:tg + 1], 1.0 / D, C_EPS, [("RF", tg)], [("RF", tg)])
            S.op("dve", lambda e, tg=tg, ot=ot: e.scalar_tensor_tensor(out=ot.ap, in0=ot.ap, scalar=RF[:, tg:tg + 1], in1=GP.ap,
                                                                       op0=ALU.mult, op1=ALU.mult),
                 r=[ot.r, ("RF", tg), GP.r], w=[ot.r])
            S.op("dve", lambda e, ot=ot, xt3=xt3: e.tensor_tensor(out=ot.ap, in0=ot.ap, in1=xt3.ap, op=ALU.add),
                 r=[ot.r, xt3.r], w=[ot.r])
            S.op("sp", lambda e, tg=tg, ot=ot: e.dma_start(out=out_d[tg * 128:(tg + 1) * 128, :], in_=ot.ap),
                 r=[ot.r], w=[("outd", tg)], dma=("x1w", p))

    phase1(x_prev)
    rglru(main=False)
    S.op("dve", lambda e: e.tensor_scalar(out=hc[:, :], in0=hc[:, :], scalar1=flag[:, 0:1], scalar2=None, op0=ALU.mult),
         r=["hc", "flag"], w=["hc"])
    S.op("dve", lambda e: e.tensor_scalar(out=xr3[:, :, :].rearrange("p n k -> p (n k)"),
                                          in0=xr3[:, :, :].rearrange("p n k -> p (n k)"),
                                          scalar1=flag[:, 0:1], scalar2=None, op0=ALU.mult),
         r=["xr3", "flag"], w=["xr3"])
    phase1(x_own)
    S.op("sp", lambda e: e.dma_start(out=GVB.ap, in_=gv_d), w=[GVB.r], dma="gvb")
    gmlp()
    rglru(main=True)
    phase3a()
    for q_ in range(8):
        S.inherit(("H2T", q_), [("YT", k) for k in range(32)])
    phase3b()
    if STOP_AFTER != "3b":
        ring_ctr[0] += (2 - ring_ctr[0]) % 4
        ffn()
        final()
    S.op("sp", None, r=[("outd", tg) for tg in range(8)] + [("oscr", tg) for tg in range(8)]
         + [("fscr", tg, c) for tg in range(8) for c in range(8)] + [("yscr", k) for k in range(32)], w=["done"])
    S.emit(nc, stack)
    stack.close()
    return nc


_NC_CACHE = {}


def _fm(v, n):
    return np.ascontiguousarray(np.asarray(v, np.float32).reshape(n, 128).T)


def kernel(x, pre_mix_g, w_in, gmlp_v_norm_g, w_spatial, b_spatial, w_conv, b_conv, w_r, b_r, w_i, b_i,
           lru_lambda, out_norm_a_g, out_norm_b_g, w_out, post_mix_g, pre_ffn_g, w_ffn_in, w_ffn_out, post_ffn_g):
    f32 = np.float32
    x = np.asarray(x, f32)
    B, SEQ, _ = x.shape
    w_in0 = np.ascontiguousarray(np.asarray(w_in, f32)[0])
    w_out0 = np.ascontiguousarray(np.asarray(w_out, f32)[0])
    w_fi0 = np.ascontiguousarray(np.asarray(w_ffn_in, f32)[0])
    w_fo0 = np.ascontiguousarray(np.asarray(w_ffn_out, f32)[0])
    pv = np.zeros((128, NV), f32)
    pv[:, 0:32] = _fm(pre_mix_g[0], 32)
    pv[:, 32:64] = _fm(pre_ffn_g[0], 32)
    wc = np.asarray(w_conv, f32)[0]
    pv[:, 64:128] = wc.reshape(4, 16, 128).transpose(2, 1, 0).reshape(128, 64)
    pv[:, 128:144] = _fm(b_conv[0], 16)
    pv[:, 144:160] = _fm(b_r[0], 16)
    pv[:, 160:176] = _fm(b_i[0], 16)
    pv[:, 176:192] = _fm(lru_lambda[0], 16)
    pv[:, 192:208] = _fm(out_norm_a_g[0], 16)
    pv[:, 208:224] = _fm(out_norm_b_g[0], 16)
    ident = np.eye(128, dtype=f32)
    tri = np.triu(np.ones((128, 128), f32))
    wsT = np.ascontiguousarray(np.asarray(w_spatial, f32)[0].transpose(2, 0, 1))
    wr = np.ascontiguousarray(np.asarray(w_r, f32)[0].transpose(1, 0, 2))
    wi = np.ascontiguousarray(np.asarray(w_i, f32)[0].transpose(1, 0, 2))
    bsp = np.ascontiguousarray(np.asarray(b_spatial, f32)[0].reshape(1, 2048))
    gvb = np.ascontiguousarray(np.broadcast_to(np.asarray(gmlp_v_norm_g, f32)[0][None, :], (128, 2048)))
    gpm = np.ascontiguousarray(np.broadcast_to(np.asarray(post_mix_g, f32)[0][None, :], (128, D)))
    gpf = np.ascontiguousarray(np.broadcast_to(np.asarray(post_ffn_g, f32)[0][None, :], (128, D)))
    zeros = np.zeros((NT, D), f32)
    in_maps = []
    for c in range(8):
        b, hf = c // 2, c % 2
        m = {
            "x_own": np.ascontiguousarray(x[b, hf * NT:(hf + 1) * NT]),
            "x_prev": np.ascontiguousarray(x[b, 0:NT]) if hf == 1 else zeros,
            "w_in": w_in0, "w_out": w_out0, "w_ffn_in": w_fi0, "w_ffn_out": w_fo0,
            "pv": pv, "flag": np.full((128, 1), float(hf), f32), "ident": ident, "tri": tri,
            "wsT": wsT, "wr": wr, "wi": wi, "bsp": bsp, "gvb": gvb, "gpm": gpm, "gpf": gpf,
        }
        in_maps.append(m)
    if "nc" not in _NC_CACHE:
        _NC_CACHE["nc"] = build_program()
    nc = _NC_CACHE["nc"]
    res = run_bass_kernel_spmd(nc, in_maps, core_ids=list(range(8)))
    out = np.empty((B, SEQ, D), f32)
    for c in range(8):
        b, hf = c // 2, c % 2
        out[b, hf * NT:(hf + 1) * NT] = np.asarray(res.results[c]["out"], f32)
    return out
```
